# Optimizing a Trainium2 kernel written in Bass

```python
import math
import jax, jax.numpy as jnp
from jax import lax
import numpy as np

D_MODEL = 1024
BATCH = 8
SEQ = 2048
DEPTH = 1
DEC_BATCH = 128
DEC_SEQ = 4
PAST_LEN = 16384
PAGE_SIZE = 128

PLE_DIM = 256
D_A = D_MODEL
D_B = D_MODEL
SSD_HEAD_DIM = 64
SSD_HEADS = D_A // SSD_HEAD_DIM
SSD_STATE = 128
SSD_GROUPS = 2
SSD_HPG = SSD_HEADS // SSD_GROUPS
SSD_CHUNK = 128
CONV_K = 4
CONV_DIM = D_A + 2 * SSD_GROUPS * SSD_STATE
S5_CH = 16
S5_GROUPS = D_B // S5_CH
S5_STATE = 64
D_IN = D_A + CONV_DIM + SSD_HEADS + D_B + D_B
EPS = 1e-6
DT_MIN = 1e-3
DT_MAX = 1e-1

kernel_name = 'hymba_ssd_s5_decode_step'


def rmsnorm(x, g):
    xf = x.astype(jnp.float32)
    y = xf * lax.rsqrt(jnp.mean(xf * xf, axis=-1, keepdims=True) + EPS)
    return (y * g.astype(jnp.float32)).astype(x.dtype)


def cmul(ar, ai, br, bi):
    return ar * br - ai * bi, ar * bi + ai * br


def causal_conv(xbc, buf, w, b):
    full = jnp.concatenate([buf.astype(xbc.dtype), xbc], axis=1)
    y = lax.conv_general_dilated(full, w[:, None, :].astype(full.dtype), window_strides=(1,),
                                 padding='VALID', dimension_numbers=('NWC', 'WIO', 'NWC'),
                                 feature_group_count=full.shape[-1])
    return jax.nn.silu(y + b.astype(y.dtype)), full[:, -(CONV_K - 1):, :]


def ssd_scan(x, dt, A, Bm, Cm, h0):
    b, T = x.shape[:2]
    L = math.gcd(T, SSD_CHUNK)
    nc = T // L
    x = x.reshape((b, nc, L) + x.shape[2:])
    dt = dt.reshape((b, nc, L) + dt.shape[2:])
    Bm = Bm.reshape((b, nc, L) + Bm.shape[2:])
    Cm = Cm.reshape((b, nc, L) + Cm.shape[2:])
    Acs = jnp.cumsum(dt * A, axis=2)
    Acs_t = jnp.transpose(Acs, (0, 1, 3, 4, 2))
    seg = Acs_t[..., :, None] - Acs_t[..., None, :]
    causal = jnp.tril(jnp.ones((L, L), dtype=bool))
    decay = jnp.exp(jnp.where(causal, seg, -jnp.inf))
    dt_t = jnp.transpose(dt, (0, 1, 3, 4, 2))
    CB = jnp.einsum('bclgn,bcsgn->bcgls', Cm, Bm)
    M = CB[:, :, :, None] * decay * dt_t[..., None, :]
    y_diag = jnp.einsum('bcgels,bcsgep->bclgep', M, x)
    w_end = jnp.exp(Acs[:, :, -1:] - Acs) * dt
    states = jnp.einsum('bclgn,bclge,bclgep->bcgepn', Bm, w_end, x)
    chunk_decay = jnp.exp(Acs[:, :, -1])

    def step(h, inp):
        s, d = inp
        return d[..., None, None] * h + s, h

    h_final, h_starts = lax.scan(step, h0, (jnp.moveaxis(states, 1, 0), jnp.moveaxis(chunk_decay, 1, 0)))
    h_starts = jnp.moveaxis(h_starts, 0, 1)
    y_off = jnp.einsum('bclgn,bcgepn,bclge->bclgep', Cm, h_starts, jnp.exp(Acs))
    y = (y_diag + y_off).reshape((b, T) + x.shape[3:])
    return y, h_final


def ssd_branch(z, xbc, dt_raw, conv_buf, h0, conv_w, conv_b, dt_bias, a_log, ssd_d, ssd_norm_g):
    f32 = jnp.float32
    b, T, _ = xbc.shape
    xbc, new_buf = causal_conv(xbc, conv_buf, conv_w, conv_b)
    xbc = xbc.astype(f32)
    xs, Bm, Cm = jnp.split(xbc, [D_A, D_A + SSD_GROUPS * SSD_STATE], axis=-1)
    xs = xs.reshape(b, T, SSD_GROUPS, SSD_HPG, SSD_HEAD_DIM)
    Bm = Bm.reshape(b, T, SSD_GROUPS, SSD_STATE)
    Cm = Cm.reshape(b, T, SSD_GROUPS, SSD_STATE)
    dt = jax.nn.softplus(dt_raw.astype(f32) + dt_bias.astype(f32)).reshape(b, T, SSD_GROUPS, SSD_HPG)
    A = -jnp.exp(a_log.astype(f32)).reshape(SSD_GROUPS, SSD_HPG)
    h0 = h0.astype(f32).reshape(b, SSD_GROUPS, SSD_HPG, SSD_HEAD_DIM, SSD_STATE)
    y, hT = ssd_scan(xs, dt, A, Bm, Cm, h0)
    y = y + ssd_d.astype(f32).reshape(SSD_GROUPS, SSD_HPG)[..., None] * xs
    gw = SSD_HPG * SSD_HEAD_DIM
    y = y.reshape(b, T, SSD_GROUPS, gw) * jax.nn.silu(z.astype(f32)).reshape(b, T, SSD_GROUPS, gw)
    y = y * lax.rsqrt(jnp.mean(y * y, axis=-1, keepdims=True) + EPS) * ssd_norm_g.astype(f32).reshape(SSD_GROUPS, gw)
    return y.reshape(b, T, D_A), new_buf, hT.reshape(b, SSD_HEADS, SSD_HEAD_DIM, SSD_STATE)


def s5_branch(u, z, h0_re, h0_im, lam_re, lam_im, log_dt, b_re, b_im, c_re, c_im, s5_d, glu_w, glu_b):
    f32 = jnp.float32
    b, T, _ = u.shape
    uf = u.astype(f32).reshape(b, T, S5_GROUPS, S5_CH)
    lr = lam_re.astype(f32)
    li = lam_im.astype(f32)
    delta = jnp.exp(log_dt.astype(f32))[:, None]
    mag = jnp.exp(lr * delta)
    ab_re = mag * jnp.cos(li * delta)
    ab_im = mag * jnp.sin(li * delta)
    q_re = ab_re - 1.0
    den = lr * lr + li * li
    f_re = (q_re * lr + ab_im * li) / den
    f_im = (ab_im * lr - q_re * li) / den
    bb_re, bb_im = cmul(f_re[..., None], f_im[..., None], b_re.astype(f32), b_im.astype(f32))
    bu_re = jnp.einsum('btgh,gph->btgp', uf, bb_re)
    bu_im = jnp.einsum('btgh,gph->btgp', uf, bb_im)
    a_re = jnp.broadcast_to(ab_re, bu_re.shape)
    a_im = jnp.broadcast_to(ab_im, bu_im.shape)

    def combine(l, r):
        a1r, a1i, b1r, b1i = l
        a2r, a2i, b2r, b2i = r
        ar, ai = cmul(a2r, a2i, a1r, a1i)
        br, bi = cmul(a2r, a2i, b1r, b1i)
        return ar, ai, br + b2r, bi + b2i

    acr, aci, xr, xi = lax.associative_scan(combine, (a_re, a_im, bu_re, bu_im), axis=1)
    pr, pi = cmul(acr, aci, h0_re.astype(f32)[:, None], h0_im.astype(f32)[:, None])
    xr = xr + pr
    xi = xi + pi
    y = (jnp.einsum('btgp,ghp->btgh', xr, c_re.astype(f32))
         - jnp.einsum('btgp,ghp->btgh', xi, c_im.astype(f32))
         + s5_d.astype(f32) * uf).reshape(b, T, D_B)
    g = jax.nn.gelu(y)
    y = g * jax.nn.sigmoid(g @ glu_w.astype(f32) + glu_b.astype(f32))
    y = y * jax.nn.silu(z.astype(f32))
    return y, xr[:, -1], xi[:, -1]


def trunk_layer(h, p, conv_buf, ssd_h, s5_re, s5_im,
                w_in, g_in, conv_w, conv_b, dt_bias, a_log, ssd_d, ssd_norm_g,
                s5_lambda_re, s5_lambda_im, s5_log_dt, s5_b_re, s5_b_im, s5_c_re, s5_c_im, s5_d,
                glu_w, glu_b, w_out, g_ple, w_ple_gate, w_ple_proj):
    n = rmsnorm(h, g_in)
    proj = n @ w_in
    i1 = D_A
    i2 = i1 + CONV_DIM
    i3 = i2 + SSD_HEADS
    i4 = i3 + D_B
    z_a, xbc, dt_raw, z_b, u_b = jnp.split(proj, [i1, i2, i3, i4], axis=-1)
    ya, conv_new, ssd_new = ssd_branch(z_a, xbc, dt_raw, conv_buf, ssd_h, conv_w, conv_b,
                                       dt_bias, a_log, ssd_d, ssd_norm_g)
    yb, s5r_new, s5i_new = s5_branch(u_b, z_b, s5_re, s5_im, s5_lambda_re, s5_lambda_im, s5_log_dt,
                                     s5_b_re, s5_b_im, s5_c_re, s5_c_im, s5_d, glu_w, glu_b)
    mix = jnp.concatenate([ya, yb], axis=-1).astype(h.dtype) @ w_out
    h = h + mix
    gate = jax.nn.sigmoid(rmsnorm(h, g_ple) @ w_ple_gate)
    h = h + (p @ w_ple_proj) * gate
    return h, conv_new, ssd_new, s5r_new, s5i_new


def setup_inputs(seed: int = 0) -> dict:
    key = jax.random.key(seed)
    ks = jax.random.split(key, 40)
    f32 = jnp.float32
    nrm = lambda k, s, sc: jax.random.normal(k, s, f32) * sc
    dt0 = jnp.exp(jax.random.uniform(ks[10], (DEPTH, SSD_HEADS), f32, math.log(DT_MIN), math.log(DT_MAX)))
    lam_im0 = jnp.pi * jnp.arange(S5_STATE, dtype=f32)
    return {
        'x_prompt': nrm(ks[0], (BATCH, SEQ, D_MODEL), 1.0),
        'x_sample': nrm(ks[1], (DEC_BATCH, DEC_SEQ, D_MODEL), 1.0),
        'p_prompt': nrm(ks[2], (DEPTH, BATCH, SEQ, PLE_DIM), 1.0),
        'p_sample': nrm(ks[3], (DEPTH, DEC_BATCH, DEC_SEQ, PLE_DIM), 1.0),
        'state_ssd': nrm(ks[4], (DEPTH, DEC_BATCH, SSD_HEADS, SSD_HEAD_DIM, SSD_STATE), 0.1),
        'state_conv': nrm(ks[5], (DEPTH, DEC_BATCH, CONV_K - 1, CONV_DIM), 1.0),
        'state_s5_re': nrm(ks[6], (DEPTH, DEC_BATCH, S5_GROUPS, S5_STATE), 0.1),
        'state_s5_im': nrm(ks[7], (DEPTH, DEC_BATCH, S5_GROUPS, S5_STATE), 0.1),
        'w_in': nrm(ks[8], (DEPTH, D_MODEL, D_IN), D_MODEL ** -0.5),
        'g_in': 1.0 + nrm(ks[9], (DEPTH, D_MODEL), 0.01),
        'conv_w': nrm(ks[11], (DEPTH, CONV_K, CONV_DIM), 0.5 * CONV_K ** -0.5),
        'conv_b': nrm(ks[12], (DEPTH, CONV_DIM), 0.01),
        'dt_bias': dt0 + jnp.log(-jnp.expm1(-dt0)),
        'a_log': jnp.log(jax.random.uniform(ks[13], (DEPTH, SSD_HEADS), f32, 1.0, 16.0)),
        'ssd_d': 1.0 + nrm(ks[14], (DEPTH, SSD_HEADS), 0.01),
        'ssd_norm_g': 1.0 + nrm(ks[15], (DEPTH, D_A), 0.01),
        's5_lambda_re': -0.5 + nrm(ks[16], (DEPTH, S5_GROUPS, S5_STATE), 0.01),
        's5_lambda_im': lam_im0 + nrm(ks[17], (DEPTH, S5_GROUPS, S5_STATE), 0.01),
        's5_log_dt': jax.random.uniform(ks[18], (DEPTH, S5_GROUPS), f32, math.log(DT_MIN), math.log(DT_MAX)),
        's5_b_re': nrm(ks[19], (DEPTH, S5_GROUPS, S5_STATE, S5_CH), (2 * S5_CH) ** -0.5),
        's5_b_im': nrm(ks[20], (DEPTH, S5_GROUPS, S5_STATE, S5_CH), (2 * S5_CH) ** -0.5),
        's5_c_re': nrm(ks[21], (DEPTH, S5_GROUPS, S5_CH, S5_STATE), S5_STATE ** -0.5),
        's5_c_im': nrm(ks[22], (DEPTH, S5_GROUPS, S5_CH, S5_STATE), S5_STATE ** -0.5),
        's5_d': nrm(ks[23], (DEPTH, S5_GROUPS, S5_CH), 1.0),
        'glu_w': nrm(ks[24], (DEPTH, D_B, D_B), D_B ** -0.5),
        'glu_b': nrm(ks[25], (DEPTH, D_B), 0.01),
        'w_out': nrm(ks[26], (DEPTH, D_A + D_B, D_MODEL), (D_A + D_B) ** -0.5),
        'g_ple': 1.0 + nrm(ks[27], (DEPTH, D_MODEL), 0.01),
        'w_ple_gate': nrm(ks[28], (DEPTH, D_MODEL, D_MODEL), D_MODEL ** -0.5),
        'w_ple_proj': nrm(ks[29], (DEPTH, PLE_DIM, D_MODEL), PLE_DIM ** -0.5),
        'g_final': 1.0 + nrm(ks[30], (D_MODEL,), 0.01),
    }


def reference(x_prompt, x_sample, p_prompt, p_sample, state_ssd, state_conv, state_s5_re, state_s5_im,
              w_in, g_in, conv_w, conv_b, dt_bias, a_log, ssd_d, ssd_norm_g,
              s5_lambda_re, s5_lambda_im, s5_log_dt, s5_b_re, s5_b_im, s5_c_re, s5_c_im, s5_d,
              glu_w, glu_b, w_out, g_ple, w_ple_gate, w_ple_proj, g_final):
    f32 = jnp.float32
    bp = x_prompt.shape[0]
    hp = x_prompt
    hs = x_sample
    ssd_p, conv_p, re_p, im_p = [], [], [], []
    ssd_s, conv_s, re_s, im_s = [], [], [], []
    for i in range(DEPTH):
        lw = (w_in[i], g_in[i], conv_w[i], conv_b[i], dt_bias[i], a_log[i], ssd_d[i], ssd_norm_g[i],
              s5_lambda_re[i], s5_lambda_im[i], s5_log_dt[i], s5_b_re[i], s5_b_im[i], s5_c_re[i],
              s5_c_im[i], s5_d[i], glu_w[i], glu_b[i], w_out[i], g_ple[i], w_ple_gate[i], w_ple_proj[i])
        hp, c_new, s_new, r_new, m_new = trunk_layer(
            hp, p_prompt[i],
            jnp.zeros((bp, CONV_K - 1, CONV_DIM), x_prompt.dtype),
            jnp.zeros((bp, SSD_HEADS, SSD_HEAD_DIM, SSD_STATE), f32),
            jnp.zeros((bp, S5_GROUPS, S5_STATE), f32),
            jnp.zeros((bp, S5_GROUPS, S5_STATE), f32),
            *lw)
        ssd_p.append(s_new)
        conv_p.append(c_new)
        re_p.append(r_new)
        im_p.append(m_new)
        hs, c_new, s_new, r_new, m_new = trunk_layer(
            hs, p_sample[i], state_conv[i], state_ssd[i], state_s5_re[i], state_s5_im[i], *lw)
        ssd_s.append(s_new)
        conv_s.append(c_new)
        re_s.append(r_new)
        im_s.append(m_new)
    y_prompt = rmsnorm(hp, g_final)
    y_sample = rmsnorm(hs, g_final)
    return (y_prompt, y_sample,
            jnp.stack(ssd_p), jnp.stack(conv_p), jnp.stack(re_p), jnp.stack(im_p),
            jnp.stack(ssd_s), jnp.stack(conv_s), jnp.stack(re_s), jnp.stack(im_s))
```

```python
import math
from contextlib import ExitStack

import numpy as np
import concourse.bass as bass
import concourse.mybir as mybir
from concourse.bass_utils import run_bass_kernel_spmd

F32 = mybir.dt.float32
BF16 = mybir.dt.bfloat16
AF = mybir.ActivationFunctionType
ALU = mybir.AluOpType
AX = mybir.AxisListType

NCORES = 8
D = 1024
SEQ = 2048
NSEQ_S = 16
TS = 4
WP = 256
NPB = SEQ // WP
WS = NSEQ_S * TS
NTOK = SEQ + WS
EPS = 1e-6
MAGIC = 12582912.0
TWO_PI = 2.0 * math.pi
NEG = -30000.0
GELU_C = 2.0 * math.sqrt(2.0 / math.pi)

NBLK_IN = 37
PV_GIN, PV_CB, PV_CW, PV_SD, PV_NG, PV_S5D, PV_GB, PV_GPLE, PV_GFIN, PV_N = 0, 8, 20, 68, 76, 84, 92, 100, 108, 116


class Sched:
    def __init__(self):
        self.ops = []
        self.state = {}
        self.last = {}

    def add(self, eng, fn, reads=(), writes=(), dma=False, extra_deps=()):
        idx = len(self.ops)
        deps = set(extra_deps)
        for k in reads:
            st = self.state.setdefault(k, {"w": [], "r": [], "pr": []})
            deps.update(st["w"])
        for k in writes:
            st = self.state.setdefault(k, {"w": [], "r": [], "pr": []})
            deps.update(st["w"])
            deps.update(st["r"])
            deps.update(st["pr"])
        for k in reads:
            self.state[k]["r"].append(idx)
        for k in writes:
            st = self.state[k]
            if st["r"]:
                st["pr"] = [r for r in st["r"] if r != idx]
                st["w"] = [idx]
                st["r"] = []
            else:
                st["w"].append(idx)
        deps.discard(idx)
        self.ops.append(dict(eng=eng, fn=fn, deps=deps, dma=dma))
        self.last[eng] = idx
        return idx

    def barrier(self):
        alld = [i for i, o in enumerate(self.ops) if o["dma"]]
        lasts = list(self.last.values())
        for eng in ("pe", "act", "dve", "pool", "sp"):
            self.add(eng, lambda e: e.nop(), extra_deps=set(alld) | set(lasts))
        self.state = {}

    def emit(self, nc, block, esem, dsems):
        ops = self.ops
        rings = {"sp": dsems[0:len(dsems) // 2], "pool": dsems[len(dsems) // 2:]}
        ndma = {"sp": 0, "pool": 0}
        dma_by_k = {}
        for i, o in enumerate(ops):
            if o["dma"]:
                o["k"] = ndma[o["eng"]]
                ndma[o["eng"]] += 1
                dma_by_k[(o["eng"], o["k"])] = i
        needed = set()
        for eng_name in ("pe", "act", "dve", "pool", "sp"):
            known = {}
            known_dma = set()
            for i, o in enumerate(ops):
                if o["eng"] != eng_name:
                    continue
                waits = []
                want = {}
                for d in o["deps"]:
                    po = ops[d]
                    if po["dma"]:
                        if d not in known_dma:
                            known_dma.add(d)
                            waits.append(("dma", d))
                    else:
                        if po["eng"] == eng_name and eng_name in ("pe", "sp"):
                            continue
                        want[po["eng"]] = max(want.get(po["eng"], -1), d)
                for pe_, d in want.items():
                    if known.get(pe_, -1) < d:
                        known[pe_] = d
                        waits.append(("eng", d))
                        needed.add(d)
                if o["dma"]:
                    k = o["k"]
                    R = len(rings[eng_name])
                    if k >= R:
                        prev = dma_by_k[(eng_name, k - R)]
                        if prev not in known_dma:
                            known_dma.add(prev)
                            waits.append(("dma", prev))
                o["waits"] = waits
        cnt = {}
        for i, o in enumerate(ops):
            if (not o["dma"]) and i in needed:
                cnt[o["eng"]] = cnt.get(o["eng"], 0) + 1
                o["cnt"] = cnt[o["eng"]]
        self.stats = dict(cnt=cnt, ndma=ndma, nops=len(ops))

        def run(eng_name):
            def body(e):
                for i, o in enumerate(ops):
                    if o["eng"] != eng_name:
                        continue
                    for kind, d in o["waits"]:
                        po = ops[d]
                        if kind == "dma":
                            ring = rings[po["eng"]]
                            e.wait_ge(ring[po["k"] % len(ring)], 16 * (po["k"] // len(ring) + 1))
                        else:
                            e.wait_ge(esem[po["eng"]], po["cnt"])
                    ins = o["fn"](e)
                    if o["dma"]:
                        ring = rings[eng_name]
                        ins.then_inc(ring[o["k"] % len(ring)], 16)
                    elif "cnt" in o:
                        ins.then_inc(esem[eng_name], 1)
            return body

        block.tensor(run("pe"))
        block.scalar(run("act"))
        block.vector(run("dve"))
        block.gpsimd(run("pool"))
        block.sync(run("sp"))


def build_program():
    nc = bass.Bass("TRN2", target_bir_lowering=False)
    S = Sched()

    def din(name, shape, dt=F32):
        return nc.dram_tensor(name, list(shape), dt, kind="ExternalInput").ap()

    def dout(name, shape, dt=F32):
        return nc.dram_tensor(name, list(shape), dt, kind="ExternalOutput").ap()

    d_xT = din("xT", [D, NTOK])
    d_pT = din("pT", [256, NTOK])
    d_win = din("w_in_l", [NBLK_IN, 128, 8, 128])
    d_glu = din("glu_l", [8, 128, 8, 128])
    d_wout = din("wout_l", [16, 128, 8, 128])
    d_wgate = din("wgate_l", [8, 128, 8, 128])
    d_wproj = din("wproj_l", [8, 128, 2, 128])
    d_pvec = din("pvec", [128, PV_N])
    d_p16 = din("p16", [16, 2])
    d_h0T = din("h0T", [NSEQ_S, 128, 1024])
    d_chist = din("chist", [128, 12, NSEQ_S, 3])
    d_winitC = din("winitC", [128, 2 * 32 * NSEQ_S])
    d_lamC = din("s5_lamC", [128, 3, 32])
    d_bC = din("s5_bC", [128, 2 * 32 * 16])
    d_cC = din("s5_cC", [128, 2 * 32 * 16])
    d_kv2 = din("kv2", [128, 2 * 16 * 32])
    d_fsel = din("fsel", [128, 4 * 8 * 128])
    d_cf = din("cf", [128, 1024])
    d_c16 = din("c16", [16, 836])
    d_cb = din("cb", [128, 1024])
    d_cs = din("cs64", [64, 80])

    def dscr(name, shape):
        return nc.dram_tensor(name, list(shape), BF16, kind="Internal").ap()
    scr = {"win": dscr("scr_win", [NBLK_IN, 128, 8, 128]), "glu": dscr("scr_glu", [8, 128, 8, 128]),
           "wout": dscr("scr_wout", [16, 128, 8, 128]), "wgate": dscr("scr_wgate", [8, 128, 8, 128]),
           "wproj": dscr("scr_wproj", [8, 128, 2, 128])}

    o_yT = dout("yT", [D, NTOK])
    o_ssdp = dout("ssd_p", [128, 1024])
    o_ssds = dout("ssd_s", [NSEQ_S, 128, 1024])
    o_convp = dout("conv_p", [128, 12, 3])
    o_convs = dout("conv_s", [128, 12, NSEQ_S, 3])
    o_s5p = dout("s5_p", [128, 2 * 32])
    o_s5s = dout("s5_s", [128, 2 * 32 * NSEQ_S])

    es = ExitStack()
    with es:
        def sb(name, shape, dt=F32):
            return es.enter_context(nc.sbuf_tensor("sb_" + name, list(shape), dt))

        psb = [es.enter_context(nc.psum_tensor("psb%d" % i, [128, 512], F32)) for i in range(8)]
        esem = {e: es.enter_context(nc.semaphore("sem_" + e)) for e in ("pe", "act", "dve", "pool", "sp")}
        dsems = [es.enter_context(nc.semaphore("dsem%d" % i)) for i in range(48)]

        ps_rr = [0]
        ps_banks = [[0, 1, 2, 3, 4, 5]]

        def ps_next():
            b = ps_banks[0][ps_rr[0] % len(ps_banks[0])]
            ps_rr[0] += 1
            return psb[b], "psb%d" % b

        def mm(out, lhsT, rhs, start, stop, r, w):
            S.add("pe", lambda e: e.matmul(out, lhsT, rhs, start=start, stop=stop), r, w)

        def tr(out, in_, ident, r, w):
            S.add("pe", lambda e: e.transpose(out, in_, ident), r, w)

        def act(out, in_, func, r, w, bias=None, scale=1.0):
            if bias is None:
                S.add("act", lambda e: e.activation(out, in_, func, scale=scale), r, w)
            else:
                S.add("act", lambda e: e.activation(out, in_, func, bias=bias, scale=scale), r, w)

        def tt(out, a, b, op, r, w):
            S.add("dve", lambda e: e.tensor_tensor(out, a, b, op), r, w)

        def ptt(out, a, b, op, r, w):
            S.add("pool", lambda e: e.tensor_tensor(out, a, b, op), r, w)

        def tsc(out, a, s1, s2, op0, op1, r, w):
            if s2 is None:
                S.add("dve", lambda e: e.tensor_scalar(out, a, s1, None, op0), r, w)
            else:
                S.add("dve", lambda e: e.tensor_scalar(out, a, s1, s2, op0, op1), r, w)

        def stt(out, a, s, b, op0, op1, r, w):
            S.add("dve", lambda e: e.scalar_tensor_tensor(out, a, s, b, op0, op1), r, w)

        def cp(out, in_, r, w, eng="dve"):
            S.add(eng, lambda e: e.tensor_copy(out, in_), r, w)

        def dma(out, in_, r, w, cast=False):
            eng = "pool" if cast else "sp"
            return S.add(eng, lambda e: e.dma_start(out=out, in_=in_), r, w, dma=True)

        cf = sb("cf", [128, 1024])
        c16 = sb("c16", [16, 836])
        cbf = sb("cbf", [128, 1024], BF16)
        cs64 = sb("cs64", [64, 80])
        pvec = sb("pvec", [128, PV_N])
        p16 = sb("p16", [16, 2])
        dma(cf[:, :], d_cf[:, :], ["d_cf"], ["cf"])
        dma(c16[:, :], d_c16[:, :], ["d_c16"], ["c16"])
        dma(cbf[:, :], d_cb[:, :], ["d_cb"], ["cbf"], cast=True)
        dma(cs64[:, :], d_cs[:, :], ["d_cs"], ["cs64"])
        dma(pvec[:, :], d_pvec[:, :], ["d_pvec"], ["pvec"])
        dma(p16[:, :], d_p16[:, :], ["d_p16"], ["p16"])
        ones_f = cf[:, 0:128]
        m1_f = cf[:, 128:256]
        m2_f = cf[:, 256:384]
        iota_p = cf[:, 384:640]
        iota_s = cf[:, 640:704]
        rmask_s = cf[:, 704:768]
        sgnA = cf[:, 768:769]
        ident_b = cbf[:, 0:128]
        maskneg_p = cbf[:, 128:640]
        maskneg_s = cbf[0:64, 640:896]
        ones_b = cbf[:, 896:1024]
        blockind = cf[0:16, 256:768].rearrange("p (a b) -> p a b", b=128)
        onesq = c16[:, 0:512].rearrange("p (a b) -> p a b", b=128)
        quadmask = c16[:, 512:516]
        reset_p = c16[:, 516:772]
        reset_s = c16[:, 772:836]
        lastmask_s = cs64[:, 0:64]
        rowmask_s = cs64[:, 64:80]

        Toep = sb("Toep", [128, 64, 128], BF16)
        Wre = sb("Wre", [128, 64, 64], BF16)
        Wim = sb("Wim", [128, 64, 64], BF16)
        CoR = sb("CoR", [128, 32, 128], BF16)
        CoI = sb("CoI", [128, 32, 128], BF16)
        Fsel = sb("Fsel", [128, 4, 8, 128], BF16)
        R8 = sb("R8", [128, 32])
        TH8 = sb("TH8", [128, 32])
        PM4r = sb("PM4r", [128, 32])
        PM4i = sb("PM4i", [128, 32])
        c1s = sb("c1s", [128, 32])
        s1s = sb("s1s", [128, 32])
        wcar = sb("wcar", [128, 2, 32])
        expA = sb("expA", [16, 1])
        dma(Fsel[:, :, :, :].rearrange("p a b c -> p (a b c)"), d_fsel[:, :], ["d_fsel"], ["Fsel"], cast=True)

        with ExitStack() as es2:
            def sb2(name, shape, dt=F32):
                return es2.enter_context(nc.sbuf_tensor("sb2_" + name, list(shape), dt))

            lamC = sb2("lamC", [128, 3, 32])
            bC = sb2("bC", [128, 2, 32, 16])
            cC = sb2("cC", [128, 2, 32, 16])
            kv2 = sb2("kv2", [128, 2, 16, 32])
            dma(lamC[:, :, :], d_lamC[:, :, :], ["d_lamC"], ["lamC"])
            dma(bC[:, :, :, :].rearrange("p a b c -> p (a b c)"), d_bC[:, :], ["d_bC"], ["bC"])
            dma(cC[:, :, :, :].rearrange("p a b c -> p (a b c)"), d_cC[:, :], ["d_cC"], ["cC"])
            dma(kv2[:, :, :, :].rearrange("p a b c -> p (a b c)"), d_kv2[:, :], ["d_kv2"], ["kv2"])
            S.add("dve", lambda e: e.memset(wcar[:, :, :], 0.0), [], ["wcar"])
            sm = {}

            def small(name):
                sm[name] = sb2("sm_" + name, [128, 32])
                return sm[name][:, :]
            lr, li, ldt = lamC[:, 0, :], lamC[:, 1, :], lamC[:, 2, :]
            dl, lrd, th, mg, kk, fr, sn, cs_, t0, t1_, fre, fim = [small(n) for n in
                ("dl", "lrd", "th", "mg", "kk", "fr", "sn", "cs", "t0", "t1", "fre", "fim")]
            K = lambda n: "sm_" + n
            act(dl, ldt, AF.Exp, ["lamC"], [K("dl")])
            tt(lrd, lr, dl, ALU.mult, ["lamC", K("dl")], [K("lrd")])
            tt(th, li, dl, ALU.mult, ["lamC", K("dl")], [K("th")])
            tsc(th, th, 1.0 / TWO_PI, None, ALU.mult, None, [K("th")], [K("th")])
            tsc(TH8[:, :], th, 8.0, None, ALU.mult, None, [K("th")], ["TH8"])
            act(mg, lrd, AF.Exp, [K("lrd")], [K("mg")])
            tsc(kk, th, MAGIC, MAGIC, ALU.add, ALU.subtract, [K("th")], [K("kk")])
            tt(fr, th, kk, ALU.subtract, [K("th"), K("kk")], [K("fr")])
            act(sn, fr, AF.Sin, [K("fr")], [K("sn")], scale=TWO_PI)
            act(kk, fr, AF.Abs, [K("fr")], [K("kk")])
            act(cs_, kk, AF.Sin, [K("kk"), "cf"], [K("cs")], bias=cf[:, 769:770], scale=-TWO_PI)
            tt(cs_, cs_, mg, ALU.mult, [K("cs"), K("mg")], [K("cs")])
            tt(sn, sn, mg, ALU.mult, [K("sn"), K("mg")], [K("sn")])
            tsc(cs_, cs_, -1.0, None, ALU.add, None, [K("cs")], [K("cs")])
            tt(mg, lr, lr, ALU.mult, ["lamC"], [K("mg")])
            tt(kk, li, li, ALU.mult, ["lamC"], [K("kk")])
            tt(mg, mg, kk, ALU.add, [K("mg"), K("kk")], [K("mg")])
            S.add("dve", lambda e: e.reciprocal(mg, mg), [K("mg")], [K("mg")])
            tt(t0, cs_, lr, ALU.mult, [K("cs"), "lamC"], [K("t0")])
            tt(t1_, sn, li, ALU.mult, [K("sn"), "lamC"], [K("t1")])
            tt(t0, t0, t1_, ALU.add, [K("t0"), K("t1")], [K("t0")])
            tt(fre, t0, mg, ALU.mult, [K("t0"), K("mg")], [K("fre")])
            tt(t0, sn, lr, ALU.mult, [K("sn"), "lamC"], [K("t0")])
            tt(t1_, cs_, li, ALU.mult, [K("cs"), "lamC"], [K("t1")])
            tt(t0, t0, t1_, ALU.subtract, [K("t0"), K("t1")], [K("t0")])
            tt(fim, t0, mg, ALU.mult, [K("t0"), K("mg")], [K("fim")])
            _tt, _tsc, _act, _cp, _mm, _stt = tt, tsc, act, cp, mm, stt
            Bbr = sb2("Bbr", [128, 32, 16])
            Bbi = sb2("Bbi", [128, 32, 16])
            tb = sb2("tb", [128, 32, 16])
            bc16 = lambda a: a.unsqueeze(2).broadcast_to([128, 32, 16])
            tt(Bbr[:, :, :], bc16(fre), bC[:, 0, :, :], ALU.mult, [K("fre"), "bC"], ["Bbr"])
            tt(tb[:, :, :], bc16(fim), bC[:, 1, :, :], ALU.mult, [K("fim"), "bC"], ["tb"])
            tt(Bbr[:, :, :], Bbr[:, :, :], tb[:, :, :], ALU.subtract, ["Bbr", "tb"], ["Bbr"])
            tt(Bbi[:, :, :], bc16(fre), bC[:, 1, :, :], ALU.mult, [K("fre"), "bC"], ["Bbi"])
            tt(tb[:, :, :], bc16(fim), bC[:, 0, :, :], ALU.mult, [K("fim"), "bC"], ["tb"])
            tt(Bbi[:, :, :], Bbi[:, :, :], tb[:, :, :], ALU.add, ["Bbi", "tb"], ["Bbi"])
            PWr = sb2("PWr", [128, 2, 16, 32])
            PWi = sb2("PWi", [128, 2, 16, 32])
            pa = sb2("pa", [128, 2, 16, 32])
            pk_ = sb2("pk", [128, 2, 16, 32])
            pf = sb2("pf", [128, 2, 16, 32])
            pm = sb2("pm", [128, 2, 16, 32])
            bck = lambda a: a.unsqueeze(1).unsqueeze(1).broadcast_to([128, 2, 16, 32])
            tt(pa[:, :, :, :], kv2[:, :, :, :], bck(th), ALU.mult, ["kv2", K("th")], ["pa"])
            tsc(pk_[:, :, :, :], pa[:, :, :, :], MAGIC, MAGIC, ALU.add, ALU.subtract, ["pa"], ["pk"])
            tt(pf[:, :, :, :], pa[:, :, :, :], pk_[:, :, :, :], ALU.subtract, ["pa", "pk"], ["pf"])
            tt(pm[:, :, :, :], kv2[:, :, :, :], bck(lrd), ALU.mult, ["kv2", K("lrd")], ["pm"])
            act(pm[:, :, :, :], pm[:, :, :, :], AF.Exp, ["pm"], ["pm"])
            act(PWi[:, :, :, :], pf[:, :, :, :], AF.Sin, ["pf"], ["PWi"], scale=TWO_PI)
            act(pk_[:, :, :, :], pf[:, :, :, :], AF.Abs, ["pf"], ["pk"])
            act(PWr[:, :, :, :], pk_[:, :, :, :], AF.Sin, ["pk", "cf"], ["PWr"], bias=cf[:, 769:770], scale=-TWO_PI)
            cp(c1s[:, :], PWr[:, 0, 15, :], ["PWr"], ["c1s"])
            cp(s1s[:, :], PWi[:, 0, 15, :], ["PWi"], ["s1s"])
            cp(R8[:, :], pm[:, 0, 15, :], ["pm"], ["R8"])
            tt(PWr[:, :, :, :], PWr[:, :, :, :], pm[:, :, :, :], ALU.mult, ["PWr", "pm"], ["PWr"])
            tt(PWi[:, :, :, :], PWi[:, :, :, :], pm[:, :, :, :], ALU.mult, ["PWi", "pm"], ["PWi"])
            cp(PM4r[:, :], PWr[:, 0, 3, :], ["PWr"], ["PM4r"])
            cp(PM4i[:, :], PWi[:, 0, 3, :], ["PWi"], ["PM4i"])


            def pw4(T, d, i0):
                return T[:, d, i0:i0 + 8, :].rearrange("p k g -> p g k").unsqueeze(3).broadcast_to([128, 32, 8, 16])
            bc8 = lambda a: a.unsqueeze(2).broadcast_to([128, 32, 8, 16])
            arr = [sb2("arr%d" % i, [128, 32, 8, 16]) for i in range(4)]
            tmpa = sb2("tmpa", [128, 32, 8, 16])

            def cmul(out_re, out_im, d, i0, xr, xi, xk, neg_im, okr, oki):
                tt(out_re, pw4(PWr, d, i0), bc8(xr), ALU.mult, ["PWr"] + xk, [okr])
                tt(tmpa[:, :, :, :], pw4(PWi, d, i0), bc8(xi), ALU.mult, ["PWi"] + xk, ["tmpa"])
                tt(out_re, out_re, tmpa[:, :, :, :], ALU.subtract, [okr, "tmpa"], [okr])
                tt(out_im, pw4(PWr, d, i0), bc8(xi), ALU.mult, ["PWr"] + xk, [oki])
                tt(tmpa[:, :, :, :], pw4(PWi, d, i0), bc8(xr), ALU.mult, ["PWi"] + xk, ["tmpa"])
                if neg_im:
                    stt(out_im, out_im, -1.0, tmpa[:, :, :, :], ALU.mult, ALU.subtract, [oki, "tmpa"], [oki])
                else:
                    tt(out_im, out_im, tmpa[:, :, :, :], ALU.add, [oki, "tmpa"], [oki])
            A4 = [a[:, :, :, :] for a in arr]
            cmul(A4[0], A4[1], 1, 8, Bbr[:, :, :], Bbi[:, :, :], ["Bbr", "Bbi"], False, "arr0", "arr1")
            cmul(A4[2], A4[3], 0, 7, cC[:, 0, :, :], cC[:, 1, :, :], ["cC"], True, "arr2", "arr3")
            maskT = cf[:, 772:900]
            ident_f = cf[:, 128:256]
            for g in range(64):
                pi_, hf = g // 2, g % 2
                rows = slice(64 * hf, 64 * hf + 64)
                ps, pk = ps_next()
                mm(ps[:, 0:128], arr[0][rows, pi_, :, :].rearrange("p a b -> p (a b)"),
                   arr[2][rows, pi_, :, :].rearrange("p a b -> p (a b)"), True, False, ["arr0", "arr2"], [pk])
                mm(ps[:, 0:128], arr[1][rows, pi_, :, :].rearrange("p a b -> p (a b)"),
                   arr[3][rows, pi_, :, :].rearrange("p a b -> p (a b)"), False, True, ["arr1", "arr3"], [pk])
                tt(Toep[:, g, :], ps[:, 0:128], maskT, ALU.mult, [pk, "cf"], ["Toep"])
            cmul(A4[0], A4[1], 1, 1, Bbr[:, :, :], Bbi[:, :, :], ["Bbr", "Bbi"], False, "arr0", "arr1")
            for g0 in range(0, 64, 8):
                for part, (src, dstW, sk) in enumerate(((arr[0], Wre, "arr0"), (arr[1], Wim, "arr1"))):
                    pss = [ps_next(), ps_next()]
                    for gi in range(8):
                        g = g0 + gi
                        pi_, hf = g // 2, g % 2
                        rows = slice(64 * hf, 64 * hf + 64)
                        ps, pk = pss[hf]
                        mm(ps[:, (gi // 2) * 64:(gi // 2 + 1) * 64], src[rows, pi_, :, :].rearrange("p a b -> p (a b)"),
                           ident_f[rows, 64 * hf:64 * hf + 64], True, True, [sk, "cf"], [pk])
                    for hf in range(2):
                        ps, pk = pss[hf]
                        act(dstW[:, g0:g0 + 8, :].rearrange("p (a b) c -> p a b c", b=2)[:, :, hf, :],
                            ps[:, 0:256].rearrange("p (a c) -> p a c", c=64), AF.Identity, [pk],
                            ["Wre" if part == 0 else "Wim"])
            cmul(A4[2], A4[3], 0, 8, cC[:, 0, :, :], cC[:, 1, :, :], ["cC"], True, "arr2", "arr3")
            cp(CoR[:, :, :], arr[2][:, :, :, :].rearrange("p g a b -> p g (a b)"), ["arr2"], ["CoR"])
            cp(CoI[:, :, :], arr[3][:, :, :, :].rearrange("p g a b -> p g (a b)"), ["arr3"], ["CoI"])
            _act(expA[:, :], p16[:, 1:2], AF.Exp, ["p16"], ["expA"])
            S.barrier()
            tt, tsc, act, cp, mm, stt = _tt, _tsc, _act, _cp, _mm, _stt

        W = WP
        xT = sb("xT", [128, 8, W])
        nT = sb("nT", [128, 8, W], BF16)
        class Rot:
            def __init__(self, name, shape, n, dt=F32):
                self.t = [(sb("%s_r%d" % (name, i), shape, dt), "%s_r%d" % (name, i)) for i in range(n)]
                self.i = 0

            def nxt(self):
                self.i += 1
                return self.t[self.i % len(self.t)]
        SQB = Rot("sqb", [128, W], 3, BF16)
        CA = Rot("cacc", [128, W], 2)
        CG = Rot("csig", [128, W], 2)
        rstd = sb("rstd", [128, W])
        wbuf = [sb("wbuf%d" % i, [128, 8, 128], BF16) for i in range(8)]
        gza = sb("gza", [128, 8, W], BF16)
        gzb = sb("gzb", [128, 8, W], BF16)
        xbc = sb("xbc", [128, 12, 3 + W])
        xs = sb("xs", [128, 8, W], BF16)
        BTb = sb("BTb", [128, 2, W], BF16)
        CTb = sb("CTb", [128, 2, W], BF16)
        uT = sb("uT", [128, 8, W], BF16)
        dtr = sb("dtr", [16, W])
        dtT = sb("dtT", [16, W])
        lndt = sb("lndt", [16, W])
        AcsT = sb("AcsT", [16, W])
        negA = sb("negA", [16, 128])
        negAq = sb("negAq", [16, 4, 128])
        bdA = sb("bdA", [16, 4, 128])
        cbt = sb("cbt", [128, 2, 128])
        eAqs = [sb("eAq%d" % i, [128, 4, 128]) for i in range(2)]
        decqs = [sb("decq%d" % i, [128, 4, 128]) for i in range(2)]
        MTqs = [sb("MTq%d" % i, [128, 4, 128], BF16) for i in range(2)]
        CTss = [sb("CTs%d" % i, [128, 4, 128], BF16) for i in range(2)]
        wendt = sb("wendt", [128, 16])
        cdt = sb("cdt", [128, 16])
        cds = sb("cds", [128, 16, NSEQ_S])
        NCH = WP // 8
        Uall = sb("Uall", [128, 8, 8, NCH], BF16)
        Yall = sb("Yall", [128, 8, 8, NCH], BF16)
        _nflat = nT[:, :, :].rearrange("p a b -> p (a b)")
        xtok = _nflat[:, 0:1024]
        xw = _nflat[:, 1024:2048]
        Btok = sb("Btok", [128, 2, 128], BF16)
        Btokm = sb("Btokm", [64, 2, 128], BF16)
        hT = sb("hT", [128, 1024])
        hTb = sb("hTb", [128, 1024], BF16)
        htmp = sb("htmp", [128, 1024])
        cstage = htmp[:, 0:12 * NSEQ_S * 3].rearrange("p (j t) -> p j t", t=NSEQ_S * 3)
        cstage2 = cstage
        yoff = Yall[:, :, :, :].rearrange("p a b c -> p (a b c)")[:, 0:2 * 8 * WS].bitcast(F32).rearrange("p (a b) -> p a b", b=WS)
        dtmp = Yall[0:64, :, :, :].rearrange("p a b c -> p (a b c)")[:, 1024:1536].bitcast(F32).rearrange("p (a b) -> p a b", b=64)
        yT = sb("yT", [128, 8, W])
        yab = sb("yab", [128, 16, W], BF16)
        gbf = xs
        pTb = sb("pTb", [128, 2, W], BF16)
        NCH = WP // 8
        def dbl(name, shape, dt=F32):
            return [sb("%s_%d" % (name, i), shape, dt) for i in range(2)]
        l_ang = [sb("l_ang", [128, 4, NCH + 1])] * 2
        l_kk = [sb("l_kk", [128, 4, NCH + 1])] * 2
        l_fr = [sb("l_fr", [128, 4, NCH + 1])] * 2
        l_cos = dbl("l_cos", [128, 4, NCH + 1])
        l_sin = dbl("l_sin", [128, 4, NCH + 1])
        l_t = [[sb("l_t%d" % i, [128, 4, NCH])] * 2 for i in range(4)]
        l_pq = [sb("l_pq%d" % i, [128, 4, NCH]) for i in range(2)]
        l_bre = dbl("l_bre", [128, 4, NCH])
        l_bim = dbl("l_bim", [128, 4, NCH])
        l_wr = dbl("l_wr", [128, 4, NCH + 1])
        l_wi = dbl("l_wi", [128, 4, NCH + 1])
        xre_all = sb("xre_all", [128, 32, NCH], BF16)
        xim_all = sb("xim_all", [128, 32, NCH], BF16)
        xi_t = dbl("xi_t", [128, 2, 4, NSEQ_S])
        sf_t = dbl("sf_t", [128, 2, 4, NSEQ_S])
        sp_t = dbl("sp_t", [128, 2, 4])

        def pv(col):
            return pvec[:, col:col + 1]
        glubh = sb("glubh", [128, 8])
        tsc(glubh[:, :], pvec[:, PV_GB:PV_GB + 8], 0.5, None, ALU.mult, None, ["pvec"], ["glubh"])

        S.add("dve", lambda e: e.memset(hT[:, :], 0.0), [], ["hT"])
        S.add("dve", lambda e: e.memset(hTb[:, :], 0.0), [], ["hTb"])
        S.add("dve", lambda e: e.memset(xbc[:, :, 0:3], 0.0), [], ["xbc"])

        out_dmas = []

        def rms_stats(src_tile, src_key, Wb, mean_n, tiles):
            ps, pk = ps_next()
            for i, j in enumerate(tiles):
                sqb, sqk = SQB.nxt()
                act(sqb[:, 0:Wb], src_tile[:, j, 0:Wb], AF.Square, [src_key % j if "%d" in src_key else src_key], [sqk])
                mm(ps[:, 0:Wb], ones_b, sqb[:, 0:Wb], i == 0, i == len(tiles) - 1, ["cbf", sqk], [pk])
            act(rstd[:, 0:Wb], ps[:, 0:Wb], AF.Ln, [pk, "cf"], ["rstd"], bias=cf[:, 770:771], scale=1.0 / mean_n)
            act(rstd[:, 0:Wb], rstd[:, 0:Wb], AF.Exp, ["rstd"], ["rstd"], scale=-0.5)

        first_pass = [True]

        def stream_matmul(wname, dsrc, nblk, nk, bufs, bkey, rhs_fn, rhs_keys, Wb, epilogue, nsplit=1, psf=None, idx=None):
            nb = len(bufs)
            nld = nblk * nsplit
            dscr_ = scr[wname]

            def load(i):
                bk_ = "%s%d" % (bkey, i % nb)
                di = idx(i) if idx is not None else i
                if first_pass[0]:
                    dma(bufs[i % nb][:, 0:nk, :], dsrc[di], ["dw"], [bk_], cast=True)
                    dma(dscr_[di], bufs[i % nb][:, 0:nk, :], [bk_], ["scr_%s_%d" % (wname, di)])
                else:
                    dma(bufs[i % nb][:, 0:nk, :], dscr_[di], ["scr_%s_%d" % (wname, di)], [bk_])
            for i in range(min(nb - 1, nld)):
                load(i)
            for j in range(nblk):
                ps, pk = psf(j) if psf is not None else ps_next()
                for h in range(nsplit):
                    i = j * nsplit + h
                    if i + nb - 1 < nld:
                        load(i + nb - 1)
                    for k in range(nk):
                        mm(ps[:, 0:Wb], bufs[i % nb][:, k, :], rhs_fn(h * nk + k), (h == 0 and k == 0),
                           (h == nsplit - 1 and k == nk - 1), ["%s%d" % (bkey, i % nb)] + rhs_keys, [pk])
                if epilogue is not None:
                    epilogue(j, ps, pk)

        def do_block(c0, Wb, sample, bidx, last_prompt):
            L = 64 if sample else 128
            nchunk = 1 if sample else Wb // 128
            for k in range(8):
                dma(xT[:, k, 0:Wb], d_xT[k * 128:(k + 1) * 128, c0:c0 + Wb], ["d_xT"], ["xT%d" % k])
            dma(pTb[:, :, 0:Wb], d_pT.rearrange("(k p) c -> p k c", p=128)[:, :, c0:c0 + Wb], ["d_pT"], ["pTb"], cast=True)
            for k in range(8):
                tsc(nT[:, k, 0:Wb], xT[:, k, 0:Wb], pv(PV_GIN + k), None, ALU.mult, None, ["xT%d" % k, "pvec"], ["nT"])
            rms_stats(xT, "xT%d", Wb, float(D), list(range(8)))
            if sample:
                dma(cstage[:, :, :], d_chist.rearrange("p j q t -> p j (q t)"), ["d_chist"], ["htmp"])
                cp(xbc[:, :, 0:NSEQ_S * 7].rearrange("p j (q t) -> p j q t", t=7)[:, :, :, 0:3],
                   cstage[:, :, :].rearrange("p j (q t) -> p j q t", t=3), ["htmp"], ["xbc"])

            def xbc_new(j):
                if sample:
                    return xbc[:, j, 0:NSEQ_S * 7].rearrange("p (q t) -> p q t", t=7)[:, :, 3:7]
                return xbc[:, j, 3:3 + Wb]

            def ep_in(j, ps, pk):
                rs = rstd[:, 0:Wb]
                if j < 8 or 21 <= j < 29:
                    cacc, cak = CA.nxt()
                    tt(cacc[:, 0:Wb], ps[:, 0:Wb], rs, ALU.mult, [pk, "rstd"], [cak])
                    if j < 8:
                        act(gza[:, j, 0:Wb], cacc[:, 0:Wb], AF.Silu, [cak], ["gza"])
                    else:
                        act(gzb[:, j - 21, 0:Wb], cacc[:, 0:Wb], AF.Silu, [cak], ["gzb"])
                elif j < 20:
                    if sample:
                        tt(xbc_new(j - 8), ps[:, 0:Wb].rearrange("p (q t) -> p q t", t=TS),
                           rs.rearrange("p (q t) -> p q t", t=TS), ALU.mult, [pk, "rstd"], ["xbc"])
                    else:
                        tt(xbc_new(j - 8), ps[:, 0:Wb], rs, ALU.mult, [pk, "rstd"], ["xbc"])
                elif j == 20:
                    tt(dtr[:, 0:Wb], ps[0:16, 0:Wb], rstd[0:16, 0:Wb], ALU.mult, [pk, "rstd"], ["dtr"])
                else:
                    tt(uT[:, j - 29, 0:Wb], ps[:, 0:Wb], rs, ALU.mult, [pk, "rstd"], ["uT"])

            stream_matmul("win", d_win, NBLK_IN, 8, wbuf, "wbuf", lambda k: nT[:, k, 0:Wb], ["nT"], Wb, ep_in)

            if sample:
                v = xbc[:, :, 0:NSEQ_S * 7].rearrange("p j (q t) -> p j q t", t=7)[:, :, :, 4:7]
                cp(cstage2[:, :, :].rearrange("p j (q t) -> p j q t", t=3), v, ["xbc"], ["htmp"])
                out_dmas.append(dma(o_convs.rearrange("p j q t -> p j (q t)"), cstage2[:, :, :], ["htmp"], ["o_convs"]))
            elif last_prompt:
                cp(cstage2[:, :, 0:3], xbc[:, :, Wb:Wb + 3], ["xbc"], ["htmp"])
                out_dmas.append(dma(o_convp[:, :, :], cstage2[:, :, 0:3], ["htmp"], ["o_convp"]))

            nch = NSEQ_S if sample else Wb // 8
            svals = list(range(4, 8)) if sample else list(range(8))
            nsv = 4 if sample else 8
            ncl = 8 * nch
            for gl in range(8):
                hb = gl // 4
                hh = slice(64 * hb, 64 * hb + 64)
                bk = 2 * hb + (gl % 4) // 2
                o_ = psb[bk][:, (gl % 2) * ncl:(gl % 2 + 1) * ncl].rearrange("p (j c) -> p j c", c=nch)
                for si, sv in enumerate(svals):
                    rhs = uT[hh, :, 0:Wb].rearrange("p j (c s) -> p j c s", s=nsv)[:, :, :, sv % nsv]
                    mm(o_, Fsel[hh, gl % 4, sv, :], rhs, si == 0, si == len(svals) - 1, ["Fsel", "uT"], ["psb%d" % bk])
            for bk in range(4):
                act(Uall[:, 2 * bk:2 * bk + 2, :, 0:nch],
                    psb[bk][:, 0:2 * ncl].rearrange("p (g j c) -> p g j c", g=2, c=nch), AF.Identity, ["psb%d" % bk], ["Uall"])
            psS, psSk = [], []
            for j in range(8):
                bk = 4 + j // 2
                psS.append(psb[bk][:, (j % 2) * 256:(j % 2) * 256 + 8 * nch].rearrange("p (a b c) -> p a b c", b=2, c=nch))
                psSk.append("psb%d" % bk)
                for gl in range(8):
                    g = 8 * j + gl
                    pr, hf = gl // 2, gl % 2
                    rows = slice(64 * hf, 64 * hf + 64)
                    mm(psS[j][rows, pr, 0, :], Wre[:, g, :], Uall[:, gl, j, 0:nch], True, True, ["Wre", "Uall"], [psSk[j]])
                    mm(psS[j][rows, pr, 1, :], Wim[:, g, :], Uall[:, gl, j, 0:nch], True, True, ["Wim", "Uall"], [psSk[j]])
            ps_banks[0] = [0, 1, 2] if sample else [0, 1, 2, 3]
            for j in range(12):
                cacc, cak = CA.nxt()

                def tap(k, j=j):
                    if sample:
                        return xbc[:, j, 0:NSEQ_S * 7].rearrange("p (q t) -> p q t", t=7)[:, :, k:k + 4]
                    return xbc[:, j, k:k + Wb]
                a3 = cacc[:, 0:Wb].rearrange("p (q t) -> p q t", t=TS) if sample else cacc[:, 0:Wb]
                tsc(a3, tap(0), pv(PV_CW + j * 4), None, ALU.mult, None, ["xbc", "pvec"], [cak])
                for k in range(1, 4):
                    stt(a3, tap(k), pv(PV_CW + j * 4 + k), a3, ALU.mult, ALU.add, ["xbc", "pvec", cak], [cak])
                if j < 8:
                    dst, dk = xs[:, j, 0:Wb], "xs"
                elif j < 10:
                    dst, dk = BTb[:, j - 8, 0:Wb], "BTb"
                else:
                    dst, dk = CTb[:, j - 10, 0:Wb], "CTb"
                act(dst, cacc[:, 0:Wb], AF.Silu, [cak, "pvec"], [dk], bias=pv(PV_CB + j))
            if not sample:
                cp(xbc[:, :, 0:3], xbc[:, :, Wb:Wb + 3], ["xbc"], ["xbc"])

            nch = NSEQ_S if sample else Wb // 8
            cbase = 0.0 if sample else float(c0 // 8)
            iota33 = cf[:, 900:933]
            def s5_level2(j):
                par = j % 2
                kx = lambda n: n if n in ("l_ang", "l_kk", "l_fr", "l_t0", "l_t1", "l_t2", "l_t3") else "%s_%d" % (n, par)
                prs = slice(4 * j, 4 * j + 4)
                Sre, Sim = psS[j][:, :, 0, :], psS[j][:, :, 1, :]
                bre, bim = l_bre[par][:, :, 0:nch], l_bim[par][:, :, 0:nch]
                t_ = [l_t[i][par][:, :, 0:nch] for i in range(4)]
                tk = [kx("l_t%d" % i) for i in range(4)]
                if sample:
                    cM = c1s[:, prs].unsqueeze(2).broadcast_to([128, 4, nch])
                    sM = s1s[:, prs].unsqueeze(2).broadcast_to([128, 4, nch])
                    tabk = ["c1s", "s1s"]
                else:
                    ang, kk_, fr_ = l_ang[par][:, :, 0:nch + 1], l_kk[par][:, :, 0:nch + 1], l_fr[par][:, :, 0:nch + 1]
                    cT, sT = l_cos[par][:, :, 0:nch + 1], l_sin[par][:, :, 0:nch + 1]
                    stt(ang, iota33[:, 0:nch + 1].unsqueeze(1).broadcast_to([128, 4, nch + 1]), cbase,
                        TH8[:, prs].unsqueeze(2).broadcast_to([128, 4, nch + 1]), ALU.add, ALU.mult, ["cf", "TH8"], [kx("l_ang")])
                    tsc(kk_, ang, MAGIC, MAGIC, ALU.add, ALU.subtract, [kx("l_ang")], [kx("l_kk")])
                    tt(fr_, ang, kk_, ALU.subtract, [kx("l_ang"), kx("l_kk")], [kx("l_fr")])
                    act(sT, fr_, AF.Sin, [kx("l_fr")], [kx("l_sin")], scale=TWO_PI)
                    act(kk_, fr_, AF.Abs, [kx("l_fr")], [kx("l_kk")])
                    act(cT, kk_, AF.Sin, [kx("l_kk"), "cf"], [kx("l_cos")], bias=cf[:, 769:770], scale=-TWO_PI)
                    cM, sM = cT[:, :, 1:nch + 1], sT[:, :, 1:nch + 1]
                    tabk = [kx("l_cos"), kx("l_sin")]
                tt(t_[0], cM, Sre, ALU.mult, tabk + [psSk[j]], [tk[0]])
                tt(t_[1], sM, Sim, ALU.mult, tabk + [psSk[j]], [tk[1]])
                tt(bre, t_[0], t_[1], ALU.add, [tk[0], tk[1]], [kx("l_bre")])
                tt(t_[2], cM, Sim, ALU.mult, tabk + [psSk[j]], [tk[2]])
                tt(t_[3], sM, Sre, ALU.mult, tabk + [psSk[j]], [tk[3]])
                tt(bim, t_[2], t_[3], ALU.subtract, [tk[2], tk[3]], [kx("l_bim")])
                wr, wi = l_wr[par], l_wi[par]
                xre, xim = xre_all[:, prs, 0:nch], xim_all[:, prs, 0:nch]
                if sample:
                    xi, xk_ = xi_t[par], kx("xi_t")
                    sf, sk_ = sf_t[par], kx("sf_t")
                    dma(xi[:, :, :, :], d_winitC.rearrange("p (a b c) -> p a b c", a=2, c=NSEQ_S)[:, :, 4 * j:4 * j + 4, :],
                        ["d_winitC"], [xk_])
                    bq = lambda a_: a_[:, prs].unsqueeze(2).broadcast_to([128, 4, NSEQ_S])
                    tt(sf[:, 0, :, :], xi[:, 1, :, :], bq(PM4i), ALU.mult, [xk_, "PM4i"], [sk_])
                    tt(sf[:, 1, :, :], xi[:, 0, :, :], bq(PM4i), ALU.mult, [xk_, "PM4i"], [sk_])
                    tt(xi[:, 0, :, :], xi[:, 0, :, :], bq(PM4r), ALU.mult, [xk_, "PM4r"], [xk_])
                    tt(xi[:, 0, :, :], xi[:, 0, :, :], sf[:, 0, :, :], ALU.subtract, [xk_, sk_], [xk_])
                    tt(xi[:, 1, :, :], xi[:, 1, :, :], bq(PM4r), ALU.mult, [xk_, "PM4r"], [xk_])
                    tt(xi[:, 1, :, :], xi[:, 1, :, :], sf[:, 1, :, :], ALU.add, [xk_, sk_], [xk_])
                    r8b = R8[:, prs].unsqueeze(2).broadcast_to([128, 4, nch])
                    tt(t_[0], xi[:, 0, :, :], r8b, ALU.mult, [xk_, "R8"], [tk[0]])
                    tt(wr[:, :, 0:nch], t_[0], bre, ALU.add, [tk[0], kx("l_bre")], [kx("l_wr")])
                    tt(t_[1], xi[:, 1, :, :], r8b, ALU.mult, [xk_, "R8"], [tk[1]])
                    tt(wi[:, :, 0:nch], t_[1], bim, ALU.add, [tk[1], kx("l_bim")], [kx("l_wi")])
                    cp(xre, xi[:, 0, :, :], [xk_], ["xre_all"])
                    cp(xim, xi[:, 1, :, :], [xk_], ["xim_all"])
                    tt(t_[2], cM, wr[:, :, 0:nch], ALU.mult, tabk + [kx("l_wr")], [tk[2]])
                    tt(t_[3], sM, wi[:, :, 0:nch], ALU.mult, tabk + [kx("l_wi")], [tk[3]])
                    tt(sf[:, 0, :, :], t_[2], t_[3], ALU.subtract, [tk[2], tk[3]], [sk_])
                    tt(t_[2], sM, wr[:, :, 0:nch], ALU.mult, tabk + [kx("l_wr")], [tk[2]])
                    tt(t_[3], cM, wi[:, :, 0:nch], ALU.mult, tabk + [kx("l_wi")], [tk[3]])
                    tt(sf[:, 1, :, :], t_[2], t_[3], ALU.add, [tk[2], tk[3]], [sk_])
                    out_dmas.append(dma(o_s5s.rearrange("p (a b c) -> p a b c", a=2, c=NSEQ_S)[:, :, 4 * j:4 * j + 4, :],
                                        sf[:, :, :, :], [sk_], ["o_s5s"]))
                else:
                    cp(wr[:, :, 0], wcar[:, 0, prs], ["wcar"], [kx("l_wr")], eng="pool")
                    cp(wi[:, :, 0], wcar[:, 1, prs], ["wcar"], [kx("l_wi")], eng="pool")
                    for pr in range(4):
                        pi_ = 4 * j + pr
                        S.add("dve", lambda e, pr=pr, pi_=pi_, wr=wr, bre=bre: e.tensor_tensor_scan(
                            wr[:, pr, 1:nch + 1], R8[:, pi_:pi_ + 1].broadcast_to([128, nch]), bre[:, pr, :], wr[:, pr, 0:1],
                            ALU.mult, ALU.add), ["R8", kx("l_bre"), kx("l_wr")], [kx("l_wr")])
                        S.add("dve", lambda e, pr=pr, pi_=pi_, wi=wi, bim=bim: e.tensor_tensor_scan(
                            wi[:, pr, 1:nch + 1], R8[:, pi_:pi_ + 1].broadcast_to([128, nch]), bim[:, pr, :], wi[:, pr, 0:1],
                            ALU.mult, ALU.add), ["R8", kx("l_bim"), kx("l_wi")], [kx("l_wi")])
                    cp(wcar[:, 0, prs], wr[:, :, nch], [kx("l_wr")], ["wcar"], eng="pool")
                    cp(wcar[:, 1, prs], wi[:, :, nch], [kx("l_wi")], ["wcar"], eng="pool")
                    cD, sD = cT[:, :, 0:nch], sT[:, :, 0:nch]
                    pq = [l_pq[i][:, :, 0:nch] for i in range(2)]
                    ptt(pq[0], cD, wr[:, :, 0:nch], ALU.mult, tabk + [kx("l_wr")], ["l_pq0"])
                    ptt(pq[1], sD, wi[:, :, 0:nch], ALU.mult, tabk + [kx("l_wi")], ["l_pq1"])
                    ptt(xre, pq[0], pq[1], ALU.subtract, ["l_pq0", "l_pq1"], ["xre_all"])
                    ptt(pq[0], sD, wr[:, :, 0:nch], ALU.mult, tabk + [kx("l_wr")], ["l_pq0"])
                    ptt(pq[1], cD, wi[:, :, 0:nch], ALU.mult, tabk + [kx("l_wi")], ["l_pq1"])
                    ptt(xim, pq[0], pq[1], ALU.add, ["l_pq0", "l_pq1"], ["xim_all"])
                    if last_prompt:
                        tt(bre[:, :, 0], cT[:, :, nch], wr[:, :, nch], ALU.mult, tabk + [kx("l_wr")], [kx("l_bre")])
                        tt(bim[:, :, 0], sT[:, :, nch], wi[:, :, nch], ALU.mult, tabk + [kx("l_wi")], [kx("l_bim")])
                        spt, spk = sp_t[par], kx("sp_t")
                        tt(spt[:, 0, :], bre[:, :, 0], bim[:, :, 0], ALU.subtract, [kx("l_bre"), kx("l_bim")], [spk])
                        tt(bre[:, :, 0], sT[:, :, nch], wr[:, :, nch], ALU.mult, tabk + [kx("l_wr")], [kx("l_bre")])
                        tt(bim[:, :, 0], cT[:, :, nch], wi[:, :, nch], ALU.mult, tabk + [kx("l_wi")], [kx("l_bim")])
                        tt(spt[:, 1, :], bre[:, :, 0], bim[:, :, 0], ALU.add, [kx("l_bre"), kx("l_bim")], [spk])
                        out_dmas.append(dma(o_s5p.rearrange("p (a b) -> p a b", a=2)[:, :, 4 * j:4 * j + 4], spt[:, :, :], [spk], ["o_s5p"]))
            jvals = list(range(4, 8)) if sample else list(range(8))

            def s5_back_Y(h):
                for j in range(4 * h, 4 * h + 4):
                    for gl in range(8):
                        g = 8 * j + gl
                        pr, hf = gl // 2, gl % 2
                        rows = slice(64 * hf, 64 * hf + 64)
                        bk = 4 + 2 * h + hf
                        o_ = psb[bk][:, (j % 4) * 4 * nch + pr * nch:(j % 4) * 4 * nch + (pr + 1) * nch]
                        mm(o_, Toep[:, g, :], Uall[:, gl, j, 0:nch], True, False, ["Toep", "Uall"], ["psb%d" % bk])
                        mm(o_, CoR[rows, g // 2, :], xre_all[rows, g // 2, 0:nch], False, False, ["CoR", "xre_all"], ["psb%d" % bk])
                        mm(o_, CoI[rows, g // 2, :], xim_all[rows, g // 2, 0:nch], False, True, ["CoI", "xim_all"], ["psb%d" % bk])
                for hf in range(2):
                    bk = 4 + 2 * h + hf
                    act(Yall[:, 4 * h:4 * h + 4, :, 0:nch].rearrange("p j (a b) c -> p j a b c", b=2)[:, :, :, hf, :],
                        psb[bk][:, 0:16 * nch].rearrange("p (j a c) -> p j a c", a=4, c=nch), AF.Identity, ["psb%d" % bk], ["Yall"])

            def s5_back_T(h):
                for jj in jvals:
                    hb = jj // 4
                    hh = slice(64 * hb, 64 * hb + 64)
                    bk = 4 + 2 * h + hb
                    o_ = psb[bk][:, (jj % 4) * 4 * nch:(jj % 4 + 1) * 4 * nch].rearrange("p (j c) -> p j c", c=nch)
                    for gl in range(8):
                        mm(o_, Fsel[hh, jj % 4, gl, :], Yall[hh, 4 * h:4 * h + 4, gl, 0:nch], gl == 0, gl == 7,
                           ["Fsel", "Yall"], ["psb%d" % bk])

            def s5_back_evac(h):
                for hb in sorted(set(jj // 4 for jj in jvals)):
                    bk = 4 + 2 * h + hb
                    if sample:
                        dst = yT[:, 4 * h:4 * h + 4, 0:Wb].rearrange("p j (c s) -> p j c s", s=TS)
                    else:
                        dst = yT[:, 4 * h:4 * h + 4, 0:Wb].rearrange("p j (c s) -> p j c s", s=8)[:, :, :, 4 * hb:4 * hb + 4]
                    src = psb[bk][:, 0:16 * nch].rearrange("p (s j c) -> p j c s", s=4, c=nch)
                    act(dst, src, AF.Identity, ["psb%d" % bk], ["yT"])
            s5_back_todo = [lambda: s5_back_Y(0), lambda: s5_back_T(0), lambda: s5_back_Y(1), lambda: s5_back_T(1)]
            s5_todo = list(range(8))
            act(dtT[:, 0:Wb], dtr[:, 0:Wb], AF.Exp, ["dtr", "p16"], ["dtT"], bias=p16[:, 0:1])
            act(dtT[:, 0:Wb], dtT[:, 0:Wb], AF.Ln, ["dtT", "cf"], ["dtT"], bias=cf[0:16, 771:772])
            act(lndt[:, 0:Wb], dtT[:, 0:Wb], AF.Ln, ["dtT"], ["lndt"])
            tsc(dtr[:, 0:Wb], dtT[:, 0:Wb], expA[:, 0:1], -1.0, ALU.mult, ALU.mult, ["dtT", "expA"], ["dtr"])
            rst = reset_s if sample else reset_p[:, 0:Wb]
            S.add("dve", lambda e: e.tensor_tensor_scan(AcsT[:, 0:Wb], rst, dtr[:, 0:Wb], 0.0, ALU.mult, ALU.add),
                  ["c16", "dtr"], ["AcsT"])
            maskneg = maskneg_s if sample else maskneg_p

            for ch in range(nchunk):
                cc = ch * 128
                cs_ = slice(cc, cc + L)
                ps, pk = ps_next()
                for gr in range(2):
                    mm(ps[0:L, gr * 128:gr * 128 + L], BTb[:, gr, cs_], CTb[:, gr, cs_], True, True, ["BTb", "CTb"], [pk])
                act(cbt[0:L, :, 0:L], ps[0:L, 0:256].rearrange("p (a b) -> p a b", b=128)[:, :, 0:L], AF.Identity, [pk], ["cbt"])
                tt(negA[:, 0:L], lndt[:, cs_], AcsT[:, cs_], ALU.subtract, ["lndt", "AcsT"], ["negA"])
                tt(negAq[:, :, 0:L], negA[:, 0:L].unsqueeze(1).broadcast_to([16, 4, L]),
                   quadmask.unsqueeze(2).broadcast_to([16, 4, L]), ALU.mult, ["negA", "c16"], ["negAq"])
                tt(bdA[:, :, 0:L], AcsT[:, cs_].unsqueeze(1).broadcast_to([16, 4, L]), blockind[:, :, 0:L], ALU.mult,
                   ["AcsT", "cf"], ["bdA"])
                ps, pk = ps_next()
                psv = ps[:, :].bitcast(BF16)
                for j in range(8):
                    tr(psv[0:L, j * 128:(j + 1) * 128], xs[:, j, cs_], ident_b, ["xs", "cbf"], [pk])
                act(xtok[0:L, :], psv[0:L, :], AF.Identity, [pk], ["nT"])
                ps, pk = ps_next()
                psv = ps[:, :].bitcast(BF16)
                for gr in range(2):
                    tr(psv[0:L, gr * 128:(gr + 1) * 128], BTb[:, gr, cs_], ident_b, ["BTb", "cbf"], [pk])
                act(Btok[0:L, :, :], psv[0:L, 0:256].rearrange("p (a b) -> p a b", b=128), AF.Identity, [pk], ["Btok"])

                yps = []
                ypcur = [None, None]

                def bufs_(q):
                    return (eAqs[q % 2], decqs[q % 2], MTqs[q % 2], CTss[q % 2],
                            "eAq%d" % (q % 2), "decq%d" % (q % 2), "MTq%d" % (q % 2), "CTs%d" % (q % 2))

                def s1(q):
                    eAq, decq, MTq, CTs, kE, kD, kM, kC = bufs_(q)
                    ps, pk = ps_next()
                    ps3 = ps[:, 0:4 * L].rearrange("p (a b) -> p a b", b=L)
                    mm(ps3, onesq[:, q, :], bdA[:, :, 0:L], True, True, ["c16", "bdA"], [pk])
                    act(eAq[:, :, 0:L], ps[:, 0:4 * L].rearrange("p (a b) -> p a b", b=L), AF.Exp, [pk], [kE])
                    ps, pk = ps_next()
                    ps3 = ps[0:L, 0:4 * L].rearrange("p (a b) -> p a b", b=L)
                    mm(ps3, negAq[:, q, 0:L], blockind[:, :, 0:L], True, False, ["negAq", "cf"], [pk])
                    mm(ps3, onesq[:, q, 0:L], bdA[:, :, 0:L], False, False, ["c16", "bdA"], [pk])
                    mm(ps3, ident_b[0:L, 0:L], maskneg[0:L, 0:4 * L].rearrange("p (a b) -> p a b", b=L), False, True, ["cbf"], [pk])
                    act(decq[0:L, :, 0:L], ps[0:L, 0:4 * L].rearrange("p (a b) -> p a b", b=L), AF.Exp, [pk], [kD])

                def s2(q):
                    gr = q // 2
                    eAq, decq, MTq, CTs, kE, kD, kM, kC = bufs_(q)
                    tt(MTq[0:L, :, 0:L], decq[0:L, :, 0:L], cbt[0:L, gr:gr + 1, 0:L].broadcast_to([L, 4, L]), ALU.mult,
                       [kD, "cbt"], [kM])
                    tt(CTs[:, :, 0:L], eAq[:, :, 0:L], CTb[:, gr:gr + 1, cs_].broadcast_to([128, 4, L]), ALU.mult,
                       [kE, "CTb"], [kC])
                    if sample:
                        tt(dtmp[:, :, :], decq[0:L, :, 0:L], lastmask_s.unsqueeze(1).broadcast_to([L, 4, L]), ALU.mult,
                           [kD, "cs64"], ["Yall"])
                        S.add("dve", lambda e, q=q: e.tensor_reduce(wendt[0:L, 4 * q:4 * q + 4], dtmp[:, :, :], AX.X, ALU.add),
                              ["Yall"], ["wendt"])
                        cp(cds[:, 4 * q:4 * q + 4, :], eAq[:, :, 0:L].rearrange("p a (q t) -> p a q t", t=TS)[:, :, :, TS - 1],
                           [kE], ["cds"])
                        cp(yab[:, q * 4:(q + 1) * 4, 0:L], CTs[:, :, 0:L], [kC], ["yab"])
                    else:
                        cp(wendt[0:L, 4 * q:4 * q + 4], decq[0:L, :, L - 1], [kD], ["wendt"])
                        cp(cdt[:, 4 * q:4 * q + 4], eAq[:, :, L - 1], [kE], ["cdt"])

                def s3(q):
                    eAq, decq, MTq, CTs, kE, kD, kM, kC = bufs_(q)
                    if q % 2 == 0:
                        ypcur[0], ypcur[1] = ps_next()
                    yp, ypk = ypcur
                    for i in range(4):
                        e_ = 4 * q + i
                        half = (e_ % 2) * 64
                        col = ((e_ // 2) % 4) * 128
                        o_ = yp[half:half + 64, col:col + L]
                        if sample:
                            mm(o_, xtok[0:L, e_ * 64:(e_ + 1) * 64], MTq[0:L, i, 0:L], True, True, ["nT", kM], [ypk])
                        else:
                            mm(o_, xtok[0:L, e_ * 64:(e_ + 1) * 64], MTq[0:L, i, 0:L], True, False, ["nT", kM], [ypk])
                            mm(o_, hTb[:, e_ * 64:(e_ + 1) * 64], CTs[:, i, 0:L], False, True, ["hTb", kC], [ypk])
                    if q % 2 == 1:
                        jb = (q // 2) * 4
                        act(yT[:, jb:jb + 4, cs_], yp[:, :].rearrange("p (a b) -> p a b", b=128)[:, :, 0:L], AF.Identity,
                            [ypk], ["yT"])
                s1(0)
                s1(1)
                for q in range(4):
                    s2(q)
                    for _ in range(2 if sample else 1):
                        if s5_todo:
                            s5_level2(s5_todo.pop(0))
                    s3(q)
                    if q + 2 < 4:
                        s1(q + 2)
                    if ch >= 1 and q in (0, 2) and len(s5_back_todo) > 2 and len(s5_todo) <= 4 - 0 and not s5_todo[:0]:
                        if all(t >= 4 for t in s5_todo):
                            s5_back_todo.pop(0)()
                tt(xw[0:L, :].rearrange("p (e c) -> p e c", c=64), xtok[0:L, :].rearrange("p (e c) -> p e c", c=64),
                   wendt[0:L, :].unsqueeze(2).broadcast_to([L, 16, 64]), ALU.mult, ["nT", "wendt"], ["nT"])
                if not sample:
                    tt(htmp[:, :].rearrange("p (e c) -> p e c", c=64), hT[:, :].rearrange("p (e c) -> p e c", c=64),
                       cdt[:, :].unsqueeze(2).broadcast_to([128, 16, 64]), ALU.mult, ["hT", "cdt"], ["htmp"])
                    for gr in range(2):
                        ps, pk = ps_next()
                        mm(ps[:, :], Btok[0:L, gr, :], xw[0:L, gr * 512:(gr + 1) * 512], True, True, ["Btok", "nT"], [pk])
                        tt(hT[:, gr * 512:(gr + 1) * 512], htmp[:, gr * 512:(gr + 1) * 512], ps[:, :], ALU.add,
                           ["htmp", pk], ["hT"])
                    act(hTb[:, :], hT[:, :], AF.Identity, ["hT"], ["hTb"])
                else:
                    yo_ps, yo_k = psb[3], "psb3"
                    xflat = xbc[:, :, :].rearrange("p a b -> p (a b)")
                    hT2, htmp2 = xflat[:, 0:1024], xflat[:, 1024:2048]
                    hTb2 = xflat[:, 2048:2560].bitcast(BF16)
                    S.add("dve", lambda e: e.memset(xflat[:, 0:1], 0.0), [], ["xbc", "hT2", "htmp2", "hTb2"])
                    sets = [(hT[:, :], hTb[:, :], htmp[:, :], "hT", "hTb", "htmp"), (hT2, hTb2, htmp2, "hT2", "hTb2", "htmp2")]
                    for sq_ in range(NSEQ_S):
                        h_f, h_b, h_t, kf, kb, kt = sets[sq_ % 2]
                        dma(h_f, d_h0T[sq_], ["d_h0T"], [kf])
                        act(h_b, h_f, AF.Identity, [kf], [kb])
                        for e_ in range(16):
                            half = (e_ % 2) * 64
                            o_ = yo_ps[half:half + 64, (e_ // 2) * 64 + sq_ * TS:(e_ // 2) * 64 + sq_ * TS + TS]
                            mm(o_, h_b[:, e_ * 64:(e_ + 1) * 64], yab[:, e_, sq_ * TS:(sq_ + 1) * TS], True, True,
                               [kb, "yab"], [yo_k])
                        tsc(Btokm[:, :, :], Btok[0:L, :, :], rowmask_s[:, sq_:sq_ + 1], None, ALU.mult, None,
                            ["Btok", "cs64"], ["Btokm"])
                        tt(h_t.rearrange("p (e c) -> p e c", c=64), h_f.rearrange("p (e c) -> p e c", c=64),
                           cds[:, :, sq_:sq_ + 1].broadcast_to([128, 16, 64]), ALU.mult, [kf, "cds"], [kt])
                        for gr in range(2):
                            ps, pk = ps_next()
                            mm(ps[:, :], Btokm[:, gr, :], xw[0:L, gr * 512:(gr + 1) * 512], True, True, ["Btokm", "nT"], [pk])
                            tt(h_t[:, gr * 512:(gr + 1) * 512], h_t[:, gr * 512:(gr + 1) * 512], ps[:, :], ALU.add,
                               [kt, pk], [kt])
                        out_dmas.append(dma(o_ssds[sq_], h_t, [kt], ["o_ssds"], cast=True))
                    act(yoff[:, :, :], yo_ps[:, :].rearrange("p (a b) -> p a b", b=WS), AF.Identity, [yo_k], ["Yall"])
                    tt(yT[:, :, 0:WS], yT[:, :, 0:WS], yoff[:, :, :], ALU.add, ["yT", "Yall"], ["yT"])
            if last_prompt:
                out_dmas.append(dma(o_ssdp[:, :], hT[:, :], ["hT"], ["o_ssdp"]))

            for j in range(8):
                stt(yT[:, j, 0:Wb], xs[:, j, 0:Wb], pv(PV_SD + j), yT[:, j, 0:Wb], ALU.mult, ALU.add,
                    ["xs", "pvec", "yT"], ["yT"])
                tt(yT[:, j, 0:Wb], yT[:, j, 0:Wb], gza[:, j, 0:Wb], ALU.mult, ["yT", "gza"], ["yT"])
            for gr in range(2):
                rms_stats(yT, "yT", Wb, 512.0, list(range(gr * 4, gr * 4 + 4)))
                if gr == 0 and not s5_todo and s5_back_todo:
                    s5_back_todo.pop(0)()
                for j in range(gr * 4, gr * 4 + 4):
                    stt(yab[:, j, 0:Wb], yT[:, j, 0:Wb], pv(PV_NG + j), rstd[:, 0:Wb], ALU.mult, ALU.mult,
                        ["yT", "pvec", "rstd"], ["yab"])

            for j in s5_todo:
                s5_level2(j)
            while s5_back_todo:
                s5_back_todo.pop(0)()
            for h in range(2):
                s5_back_evac(h)
            for j in range(8):
                stt(yT[:, j, 0:Wb], uT[:, j, 0:Wb], pv(PV_S5D + j), yT[:, j, 0:Wb], ALU.mult, ALU.add,
                    ["uT", "pvec", "yT"], ["yT"])
            ps_banks[0] = [0, 1, 2, 3, 4, 5]

            def ep_out(j, ps, pk):
                tt(xT[:, j, 0:Wb], ps[:, 0:Wb], xT[:, j, 0:Wb], ALU.add, [pk, "xT%d" % j], ["xT%d" % j])
            stream_matmul("wout", d_wout, 8, 8, wbuf, "wbuf", lambda k: yab[:, k, 0:Wb], ["yab"], Wb, ep_out, idx=lambda i: 2 * i)
            for j in range(8):
                cacc, cak = CA.nxt()
                csig, cgk = CG.nxt()
                act(cacc[:, 0:Wb], yT[:, j, 0:Wb], AF.Square, ["yT"], [cak])
                tsc(cacc[:, 0:Wb], cacc[:, 0:Wb], 0.044715, 1.0, ALU.mult, ALU.add, [cak], [cak])
                tt(cacc[:, 0:Wb], cacc[:, 0:Wb], yT[:, j, 0:Wb], ALU.mult, [cak, "yT"], [cak])
                act(csig[:, 0:Wb], cacc[:, 0:Wb], AF.Tanh, [cak], [cgk], scale=0.5 * GELU_C)
                stt(yT[:, j, 0:Wb], csig[:, 0:Wb], 1.0, yT[:, j, 0:Wb], ALU.add, ALU.mult, ["yT", cgk], ["yT"])
                cp(gbf[:, j, 0:Wb], yT[:, j, 0:Wb], ["yT"], ["xs"])

            def ep_glu(j, ps, pk):
                cacc, cak = CA.nxt()
                csig, cgk = CG.nxt()
                act(csig[:, 0:Wb], ps[:, 0:Wb], AF.Tanh, [pk, "glubh"], [cgk], bias=glubh[:, j:j + 1], scale=0.25)
                stt(cacc[:, 0:Wb], csig[:, 0:Wb], 1.0, yT[:, j, 0:Wb], ALU.add, ALU.mult, ["yT", cgk], [cak])
                stt(yab[:, 8 + j, 0:Wb], cacc[:, 0:Wb], 0.25, gzb[:, j, 0:Wb], ALU.mult, ALU.mult, [cak, "gzb"], ["yab"])
            stream_matmul("glu", d_glu, 8, 8, wbuf, "wbuf", lambda k: gbf[:, k, 0:Wb], ["xs"], Wb, ep_glu)

            stream_matmul("wout", d_wout, 8, 8, wbuf, "wbuf", lambda k: yab[:, 8 + k, 0:Wb], ["yab"], Wb, ep_out, idx=lambda i: 2 * i + 1)

            ps_banks[0] = [0, 1, 2, 3]

            def proj_ps(j):
                bk = 4 + j // 2
                return psb[bk][:, (j % 2) * 256:(j % 2) * 256 + 256], "psb%d" % bk
            stream_matmul("wproj", d_wproj, 8, 2, wbuf, "wbuf", lambda k: pTb[:, k, 0:Wb], ["pTb"], Wb, None, psf=proj_ps)
            for k in range(8):
                tsc(nT[:, k, 0:Wb], xT[:, k, 0:Wb], pv(PV_GPLE + k), None, ALU.mult, None, ["xT%d" % k, "pvec"], ["nT"])
            rms_stats(xT, "xT%d", Wb, float(D), list(range(8)))

            def ep_gate(j, ps, pk):
                cacc, cak = CA.nxt()
                csig, cgk = CG.nxt()
                pp, ppk = proj_ps(j)
                tt(cacc[:, 0:Wb], ps[:, 0:Wb], rstd[:, 0:Wb], ALU.mult, [pk, "rstd"], [cak])
                act(csig[:, 0:Wb], cacc[:, 0:Wb], AF.Tanh, [cak], [cgk], scale=0.5)
                stt(cacc[:, 0:Wb], csig[:, 0:Wb], 1.0, pp[:, 0:Wb], ALU.add, ALU.mult, [cgk, ppk], [cak])
                stt(xT[:, j, 0:Wb], cacc[:, 0:Wb], 0.5, xT[:, j, 0:Wb], ALU.mult, ALU.add, [cak, "xT%d" % j], ["xT%d" % j])
            stream_matmul("wgate", d_wgate, 8, 8, wbuf, "wbuf", lambda k: nT[:, k, 0:Wb], ["nT"], Wb, ep_gate)
            ps_banks[0] = [0, 1, 2, 3, 4, 5]

            rms_stats(xT, "xT%d", Wb, float(D), list(range(8)))
            for k in range(8):
                stt(yT[:, k, 0:Wb], xT[:, k, 0:Wb], pv(PV_GFIN + k), rstd[:, 0:Wb], ALU.mult, ALU.mult,
                    ["xT%d" % k, "pvec", "rstd"], ["yT"])
            for k in range(8):
                out_dmas.append(dma(o_yT[k * 128:(k + 1) * 128, c0:c0 + Wb], yT[:, k, 0:Wb], ["yT"], ["o_yT"]))

        nblk = NPB
        for b in range(nblk):
            do_block(b * WP, WP, False, b, b == NPB - 1)
            first_pass[0] = False
        do_block(SEQ, WS, True, NPB, False)

        S.add("sp", lambda e: e.nop(), extra_deps=set(out_dmas))

        with nc.Block() as block:
            S.emit(nc, block, esem, dsems)
        build_program.stats = S.stats
    return nc


def _consts():
    cf = np.zeros((128, 1024), np.float32)
    cf[:, 0:128] = 1.0
    cf[:, 128:256] = np.eye(128, dtype=np.float32)
    cf[0:64, 768] = 1.0
    cf[64:128, 768] = -1.0
    cf[:, 769] = math.pi / 2.0
    cf[:, 770] = EPS
    cf[:, 771] = 1.0
    sj = np.arange(128) // 16
    cf[:, 772:900] = (sj[None, :] >= sj[:, None]).astype(np.float32)
    cf[:, 900:933] = np.arange(33, dtype=np.float32)[None, :]
    c16 = np.zeros((16, 836), np.float32)
    bi = np.zeros((16, 4, 128), np.float32)
    oq = np.zeros((16, 4, 128), np.float32)
    for e in range(16):
        bi[e, e % 4, :] = 1.0
        oq[e, e // 4, :] = 1.0
        c16[e, 512 + e // 4] = 1.0
    cf[0:16, 256:768] = bi.reshape(16, 512)
    c16[:, 0:512] = oq.reshape(16, 512)
    rp = np.ones(256, np.float32)
    rp[0] = 0.0
    rp[128] = 0.0
    c16[:, 516:772] = rp[None, :]
    c16[:, 772:836] = (np.arange(64) % TS != 0).astype(np.float32)[None, :]
    cb = np.zeros((128, 1024), np.float32)
    cb[:, 896:1024] = 1.0
    cb[:, 0:128] = np.eye(128, dtype=np.float32)
    s_ = np.arange(128)[:, None]
    t_ = np.arange(128)[None, :]
    mp = np.where(s_ <= t_, 0.0, NEG).astype(np.float32)
    cb[:, 128:640] = np.tile(mp, (1, 4))
    s6 = np.arange(64)[:, None]
    t6 = np.arange(64)[None, :]
    ms = np.where((s6 <= t6) & (s6 // TS == t6 // TS), 0.0, NEG).astype(np.float32)
    cb[0:64, 640:896] = np.tile(ms, (1, 4))
    cs = np.zeros((64, 80), np.float32)
    cs[:, 0:64] = (t6 == (s6 // TS) * TS + TS - 1).astype(np.float32)
    cs[:, 64:80] = (s6 // TS == np.arange(16)[None, :]).astype(np.float32)
    return cf, c16, cb, cs


def _wblocks(w, nk):
    ncols = w.shape[1]
    return np.ascontiguousarray(w.reshape(nk, 128, ncols // 128, 128).transpose(2, 1, 0, 3))


def _chan(v):
    return np.ascontiguousarray(np.asarray(v, np.float32).reshape(8, 128).T)


_PROG = {}


def kernel(x_prompt, x_sample, p_prompt, p_sample, state_ssd, state_conv, state_s5_re, state_s5_im,
           w_in, g_in, conv_w, conv_b, dt_bias, a_log, ssd_d, ssd_norm_g,
           s5_lambda_re, s5_lambda_im, s5_log_dt, s5_b_re, s5_b_im, s5_c_re, s5_c_im, s5_d,
           glu_w, glu_b, w_out, g_ple, w_ple_gate, w_ple_proj, g_final):
    f = lambda a: np.asarray(a, np.float32)
    x_prompt, x_sample, p_prompt, p_sample = f(x_prompt), f(x_sample), f(p_prompt)[0], f(p_sample)[0]
    state_ssd, state_conv = f(state_ssd)[0], f(state_conv)[0]
    s5re, s5im = f(state_s5_re)[0], f(state_s5_im)[0]
    w_in0 = f(w_in)[0]
    wcols = np.zeros((1024, NBLK_IN * 128), np.float32)
    wcols[:, 0:2560] = w_in0[:, 0:2560]
    wcols[:, 2560:2576] = w_in0[:, 2560:2576]
    wcols[:, 2688:4736] = w_in0[:, 2576:4624]
    w_in_l = _wblocks(wcols, 8)
    glu_l = _wblocks(f(glu_w)[0], 8)
    wo = _wblocks(f(w_out)[0], 16)
    wout_l = np.ascontiguousarray(wo.reshape(8, 128, 2, 8, 128).transpose(0, 2, 1, 3, 4).reshape(16, 128, 8, 128))
    wgate_l = _wblocks(f(w_ple_gate)[0], 8)
    wproj_l = _wblocks(f(w_ple_proj)[0], 2)
    pvec = np.zeros((128, PV_N), np.float32)
    pvec[:, PV_GIN:PV_GIN + 8] = _chan(f(g_in)[0])
    cbv = f(conv_b)[0].reshape(12, 128).T
    pvec[:, PV_CB:PV_CB + 12] = cbv
    cw = f(conv_w)[0].reshape(4, 12, 128).transpose(2, 1, 0)
    pvec[:, PV_CW:PV_CW + 48] = cw.reshape(128, 48)
    pvec[:, PV_SD:PV_SD + 8] = _chan(np.repeat(f(ssd_d)[0], 64))
    pvec[:, PV_NG:PV_NG + 8] = _chan(f(ssd_norm_g)[0])
    pvec[:, PV_S5D:PV_S5D + 8] = _chan(f(s5_d)[0].reshape(-1))
    pvec[:, PV_GB:PV_GB + 8] = _chan(f(glu_b)[0])
    pvec[:, PV_GPLE:PV_GPLE + 8] = _chan(f(g_ple)[0])
    pvec[:, PV_GFIN:PV_GFIN + 8] = _chan(f(g_final))
    p16 = np.stack([f(dt_bias)[0], f(a_log)[0]], axis=1).astype(np.float32)
    lre, lim, ldt = f(s5_lambda_re)[0], f(s5_lambda_im)[0], f(s5_log_dt)[0]
    bre, bim = f(s5_b_re)[0], f(s5_b_im)[0]
    cre, cim = f(s5_c_re)[0], f(s5_c_im)[0]
    def layC(arr_gp):
        a_ = np.asarray(arr_gp, np.float32)
        a_ = a_.reshape((32, 2) + a_.shape[1:])
        a_ = np.moveaxis(a_, 0, 2)
        return np.ascontiguousarray(a_.reshape((128, 32) + a_.shape[3:]))
    lamC = np.stack([layC(lre), layC(lim), layC(np.repeat(ldt[:, None], 64, axis=1))], axis=1)
    bC = np.stack([layC(bre), layC(bim)], axis=1).reshape(128, -1)
    cC = np.stack([layC(cre.transpose(0, 2, 1)), layC(cim.transpose(0, 2, 1))], axis=1).reshape(128, -1)
    kv2 = np.zeros((128, 2, 16, 32), np.float32)
    kv2[:, 0] = (np.arange(16) - 7).astype(np.float32)[None, :, None]
    kv2[:, 1] = (8 - np.arange(16)).astype(np.float32)[None, :, None]
    kv2 = kv2.reshape(128, -1)
    fsel = np.zeros((128, 4, 8, 128), np.float32)
    for hf_ in range(2):
        for a_ in range(4):
            for b_ in range(8):
                for h_ in range(16):
                    fsel[hf_ * 64 + a_ * 16 + h_, a_, b_, b_ * 16 + h_] = 1.0
    fsel = fsel.reshape(128, -1)
    cf, c16, cb, cs = _consts()

    in_maps = []
    for c in range(NCORES):
        qs = slice(c * NSEQ_S, (c + 1) * NSEQ_S)
        xT = np.concatenate([x_prompt[c].T, x_sample[qs].reshape(WS, D).T], axis=1)
        pT = np.concatenate([p_prompt[c].T, p_sample[qs].reshape(WS, 256).T], axis=1)
        h0T = state_ssd[qs].reshape(NSEQ_S, 1024, 128).transpose(0, 2, 1)
        chist = state_conv[qs].reshape(NSEQ_S, 3, 12, 128).transpose(3, 2, 0, 1)
        winitC = np.stack([layC(s5re[qs].transpose(1, 2, 0)), layC(s5im[qs].transpose(1, 2, 0))], axis=1).reshape(128, -1)
        in_maps.append({
            "xT": np.ascontiguousarray(xT), "pT": np.ascontiguousarray(pT),
            "w_in_l": w_in_l, "glu_l": glu_l, "wout_l": wout_l, "wgate_l": wgate_l, "wproj_l": wproj_l,
            "pvec": pvec, "p16": p16,
            "h0T": np.ascontiguousarray(h0T), "chist": np.ascontiguousarray(chist),
            "winitC": np.ascontiguousarray(winitC),
            "s5_lamC": lamC, "s5_bC": bC, "s5_cC": cC, "kv2": kv2, "fsel": fsel,
            "cf": cf, "c16": c16, "cb": cb, "cs64": cs,
        })
    if "nc" not in _PROG:
        _PROG["nc"] = build_program()
    res = run_bass_kernel_spmd(_PROG["nc"], in_maps, core_ids=list(range(NCORES)))
    R = res.results

    y_prompt = np.stack([R[c]["yT"][:, 0:SEQ].T for c in range(NCORES)], axis=0)
    y_sample = np.concatenate([R[c]["yT"][:, SEQ:].T.reshape(NSEQ_S, TS, D) for c in range(NCORES)], axis=0)
    ssd_p = np.stack([R[c]["ssd_p"].T.reshape(16, 64, 128) for c in range(NCORES)], axis=0)[None]
    ssd_s = np.concatenate([R[c]["ssd_s"].transpose(0, 2, 1).reshape(NSEQ_S, 16, 64, 128) for c in range(NCORES)], axis=0)[None]
    conv_p = np.stack([R[c]["conv_p"].transpose(2, 1, 0).reshape(3, 1536) for c in range(NCORES)], axis=0)[None]
    conv_s = np.concatenate([R[c]["conv_s"].transpose(2, 3, 1, 0).reshape(NSEQ_S, 3, 1536) for c in range(NCORES)], axis=0)[None]
    def unC(x):
        x = x.reshape((2, 64, 32) + x.shape[2:])
        x = np.moveaxis(x, 2, 0)
        return x.reshape((64, 64) + x.shape[3:])
    re_p = np.stack([unC(R[c]["s5_p"].reshape(128, 2, 32)[:, 0]) for c in range(NCORES)], axis=0)[None]
    im_p = np.stack([unC(R[c]["s5_p"].reshape(128, 2, 32)[:, 1]) for c in range(NCORES)], axis=0)[None]
    re_s = np.concatenate([unC(R[c]["s5_s"].reshape(128, 2, 32, NSEQ_S)[:, 0]).transpose(2, 0, 1) for c in range(NCORES)], axis=0)[None]
    im_s = np.concatenate([unC(R[c]["s5_s"].reshape(128, 2, 32, NSEQ_S)[:, 1]).transpose(2, 0, 1) for c in range(NCORES)], axis=0)[None]
    c_ = np.ascontiguousarray
    return (c_(y_prompt.astype(np.float32)), c_(y_sample.astype(np.float32)), c_(ssd_p), c_(conv_p), c_(re_p), c_(im_p),
            c_(ssd_s), c_(conv_s), c_(re_s), c_(im_s))
```

```python
import math
from contextlib import ExitStack

import numpy as np
import concourse.bass as bass
import concourse.mybir as mybir
from concourse.bass_utils import run_bass_kernel_spmd

F32 = mybir.dt.float32
BF16 = mybir.dt.bfloat16
AF = mybir.ActivationFunctionType
ALU = mybir.AluOpType
AX = mybir.AxisListType

NCORES = 8
D = 1024
SEQ = 2048
NSEQ_S = 16
TS = 4
WP = 256
NPB = SEQ // WP
WS = NSEQ_S * TS
NTOK = SEQ + WS
EPS = 1e-6
MAGIC = 12582912.0
TWO_PI = 2.0 * math.pi
NEG = -30000.0
GELU_C = 2.0 * math.sqrt(2.0 / math.pi)

NBLK_IN = 37
PV_GIN, PV_CB, PV_CW, PV_SD, PV_NG, PV_S5D, PV_GB, PV_GPLE, PV_GFIN, PV_N = 0, 8, 20, 68, 76, 84, 92, 100, 108, 116


class Sched:
    def __init__(self):
        self.ops = []
        self.state = {}
        self.last = {}

    def add(self, eng, fn, reads=(), writes=(), dma=False, extra_deps=()):
        idx = len(self.ops)
        deps = set(extra_deps)
        for k in reads:
            st = self.state.setdefault(k, {"w": [], "r": [], "pr": []})
            deps.update(st["w"])
        for k in writes:
            st = self.state.setdefault(k, {"w": [], "r": [], "pr": []})
            deps.update(st["w"])
            deps.update(st["r"])
            deps.update(st["pr"])
        for k in reads:
            self.state[k]["r"].append(idx)
        for k in writes:
            st = self.state[k]
            if st["r"]:
                st["pr"] = [r for r in st["r"] if r != idx]
                st["w"] = [idx]
                st["r"] = []
            else:
                st["w"].append(idx)
        deps.discard(idx)
        self.ops.append(dict(eng=eng, fn=fn, deps=deps, dma=dma))
        self.last[eng] = idx
        return idx

    def barrier(self):
        alld = [i for i, o in enumerate(self.ops) if o["dma"]]
        lasts = list(self.last.values())
        for eng in ("pe", "act", "dve", "pool", "sp"):
            self.add(eng, lambda e: e.nop(), extra_deps=set(alld) | set(lasts))
        self.state = {}

    def emit(self, nc, block, esem, dsems):
        ops = self.ops
        rings = {"sp": dsems[0:len(dsems) // 2], "pool": dsems[len(dsems) // 2:]}
        ndma = {"sp": 0, "pool": 0}
        dma_by_k = {}
        for i, o in enumerate(ops):
            if o["dma"]:
                o["k"] = ndma[o["eng"]]
                ndma[o["eng"]] += 1
                dma_by_k[(o["eng"], o["k"])] = i
        needed = set()
        for eng_name in ("pe", "act", "dve", "pool", "sp"):
            known = {}
            known_dma = set()
            for i, o in enumerate(ops):
                if o["eng"] != eng_name:
                    continue
                waits = []
                want = {}
                for d in o["deps"]:
                    po = ops[d]
                    if po["dma"]:
                        if d not in known_dma:
                            known_dma.add(d)
                            waits.append(("dma", d))
                    else:
                        if po["eng"] == eng_name and eng_name in ("pe", "sp"):
                            continue
                        want[po["eng"]] = max(want.get(po["eng"], -1), d)
                for pe_, d in want.items():
                    if known.get(pe_, -1) < d:
                        known[pe_] = d
                        waits.append(("eng", d))
                        needed.add(d)
                if o["dma"]:
                    k = o["k"]
                    R = len(rings[eng_name])
                    if k >= R:
                        prev = dma_by_k[(eng_name, k - R)]
                        if prev not in known_dma:
                            known_dma.add(prev)
                            waits.append(("dma", prev))
                o["waits"] = waits
        cnt = {}
        for i, o in enumerate(ops):
            if (not o["dma"]) and i in needed:
                cnt[o["eng"]] = cnt.get(o["eng"], 0) + 1
                o["cnt"] = cnt[o["eng"]]
        self.stats = dict(cnt=cnt, ndma=ndma, nops=len(ops))

        def run(eng_name):
            def body(e):
                for i, o in enumerate(ops):
                    if o["eng"] != eng_name:
                        continue
                    for kind, d in o["waits"]:
                        po = ops[d]
                        if kind == "dma":
                            ring = rings[po["eng"]]
                            e.wait_ge(ring[po["k"] % len(ring)], 16 * (po["k"] // len(ring) + 1))
                        else:
                            e.wait_ge(esem[po["eng"]], po["cnt"])
                    ins = o["fn"](e)
                    if o["dma"]:
                        ring = rings[eng_name]
                        ins.then_inc(ring[o["k"] % len(ring)], 16)
                    elif "cnt" in o:
                        ins.then_inc(esem[eng_name], 1)
            return body

        block.tensor(run("pe"))
        block.scalar(run("act"))
        block.vector(run("dve"))
        block.gpsimd(run("pool"))
        block.sync(run("sp"))


def build_program():
    nc = bass.Bass("TRN2", target_bir_lowering=False)
    S = Sched()

    def din(name, shape, dt=F32):
        return nc.dram_tensor(name, list(shape), dt, kind="ExternalInput").ap()

    def dout(name, shape, dt=F32):
        return nc.dram_tensor(name, list(shape), dt, kind="ExternalOutput").ap()

    d_xT = din("xT", [D, NTOK])
    d_pT = din("pT", [256, NTOK])
    d_win = din("w_in_l", [NBLK_IN, 128, 8, 128])
    d_glu = din("glu_l", [8, 128, 8, 128])
    d_wout = din("wout_l", [16, 128, 8, 128])
    d_wgate = din("wgate_l", [8, 128, 8, 128])
    d_wproj = din("wproj_l", [8, 128, 2, 128])
    d_pvec = din("pvec", [128, PV_N])
    d_p16 = din("p16", [16, 2])
    d_h0T = din("h0T", [NSEQ_S, 128, 1024])
    d_chist = din("chist", [128, 12, NSEQ_S, 3])
    d_winitC = din("winitC", [128, 2 * 32 * NSEQ_S])
    d_lamC = din("s5_lamC", [128, 3, 32])
    d_bC = din("s5_bC", [128, 2 * 32 * 16])
    d_cC = din("s5_cC", [128, 2 * 32 * 16])
    d_kv2 = din("kv2", [128, 2 * 16 * 32])
    d_fsel = din("fsel", [128, 4 * 8 * 128])
    d_cf = din("cf", [128, 1024])
    d_c16 = din("c16", [16, 836])
    d_cb = din("cb", [128, 1024])
    d_cs = din("cs64", [64, 80])

    def dscr(name, shape):
        return nc.dram_tensor(name, list(shape), BF16, kind="Internal").ap()
    scr = {"win": dscr("scr_win", [NBLK_IN, 128, 8, 128]), "glu": dscr("scr_glu", [8, 128, 8, 128]),
           "wout": dscr("scr_wout", [16, 128, 8, 128]), "wgate": dscr("scr_wgate", [8, 128, 8, 128]),
           "wproj": dscr("scr_wproj", [8, 128, 2, 128])}

    o_yT = dout("yT", [D, NTOK])
    o_ssdp = dout("ssd_p", [128, 1024])
    o_ssds = dout("ssd_s", [NSEQ_S, 128, 1024])
    o_convp = dout("conv_p", [128, 12, 3])
    o_convs = dout("conv_s", [128, 12, NSEQ_S, 3])
    o_s5p = dout("s5_p", [128, 2 * 32])
    o_s5s = dout("s5_s", [128, 2 * 32 * NSEQ_S])

    es = ExitStack()
    with es:
        def sb(name, shape, dt=F32):
            return es.enter_context(nc.sbuf_tensor("sb_" + name, list(shape), dt))

        psb = [es.enter_context(nc.psum_tensor("psb%d" % i, [128, 512], F32)) for i in range(8)]
        esem = {e: es.enter_context(nc.semaphore("sem_" + e)) for e in ("pe", "act", "dve", "pool", "sp")}
        dsems = [es.enter_context(nc.semaphore("dsem%d" % i)) for i in range(48)]

        ps_rr = [0]
        ps_banks = [[0, 1, 2, 3, 4, 5, 6, 7]]

        def ps_next():
            b = ps_banks[0][ps_rr[0] % len(ps_banks[0])]
            ps_rr[0] += 1
            return psb[b], "psb%d" % b

        def mm(out, lhsT, rhs, start, stop, r, w):
            S.add("pe", lambda e: e.matmul(out, lhsT, rhs, start=start, stop=stop), r, w)

        def tr(out, in_, ident, r, w):
            S.add("pe", lambda e: e.transpose(out, in_, ident), r, w)

        def act(out, in_, func, r, w, bias=None, scale=1.0):
            if bias is None:
                S.add("act", lambda e: e.activation(out, in_, func, scale=scale), r, w)
            else:
                S.add("act", lambda e: e.activation(out, in_, func, bias=bias, scale=scale), r, w)

        def tt(out, a, b, op, r, w):
            S.add("dve", lambda e: e.tensor_tensor(out, a, b, op), r, w)

        def ptt(out, a, b, op, r, w):
            S.add("pool", lambda e: e.tensor_tensor(out, a, b, op), r, w)

        def tsc(out, a, s1, s2, op0, op1, r, w):
            if s2 is None:
                S.add("dve", lambda e: e.tensor_scalar(out, a, s1, None, op0), r, w)
            else:
                S.add("dve", lambda e: e.tensor_scalar(out, a, s1, s2, op0, op1), r, w)

        def stt(out, a, s, b, op0, op1, r, w):
            S.add("dve", lambda e: e.scalar_tensor_tensor(out, a, s, b, op0, op1), r, w)

        def cp(out, in_, r, w, eng="dve"):
            S.add(eng, lambda e: e.tensor_copy(out, in_), r, w)

        def dma(out, in_, r, w, cast=False):
            eng = "pool" if cast else "sp"
            return S.add(eng, lambda e: e.dma_start(out=out, in_=in_), r, w, dma=True)

        cf = sb("cf", [128, 1024])
        c16 = sb("c16", [16, 836])
        cbf = sb("cbf", [128, 1024], BF16)
        cs64 = sb("cs64", [64, 80])
        pvec = sb("pvec", [128, PV_N])
        p16 = sb("p16", [16, 2])
        dma(cf[:, :], d_cf[:, :], ["d_cf"], ["cf"])
        dma(c16[:, :], d_c16[:, :], ["d_c16"], ["c16"])
        dma(cbf[:, :], d_cb[:, :], ["d_cb"], ["cbf"], cast=True)
        dma(cs64[:, :], d_cs[:, :], ["d_cs"], ["cs64"])
        dma(pvec[:, :], d_pvec[:, :], ["d_pvec"], ["pvec"])
        dma(p16[:, :], d_p16[:, :], ["d_p16"], ["p16"])
        ones_f = cf[:, 0:128]
        m1_f = cf[:, 128:256]
        m2_f = cf[:, 256:384]
        iota_p = cf[:, 384:640]
        iota_s = cf[:, 640:704]
        rmask_s = cf[:, 704:768]
        sgnA = cf[:, 768:769]
        ident_b = cbf[:, 0:128]
        maskneg_p = cbf[:, 128:640]
        maskneg_s = cbf[0:64, 640:896]
        ones_b = cbf[:, 896:1024]
        blockind = cf[0:16, 256:768].rearrange("p (a b) -> p a b", b=128)
        onesq = c16[:, 0:512].rearrange("p (a b) -> p a b", b=128)
        quadmask = c16[:, 512:516]
        reset_p = c16[:, 516:772]
        reset_s = c16[:, 772:836]
        lastmask_s = cs64[:, 0:64]
        rowmask_s = cs64[:, 64:80]

        Toep = sb("Toep", [128, 64, 128], BF16)
        Wre = sb("Wre", [128, 64, 64], BF16)
        Wim = sb("Wim", [128, 64, 64], BF16)
        CoR = sb("CoR", [128, 32, 128], BF16)
        CoI = sb("CoI", [128, 32, 128], BF16)
        Fsel = sb("Fsel", [128, 4, 8, 128], BF16)
        R8 = sb("R8", [128, 32])
        TH8 = sb("TH8", [128, 32])
        PM4r = sb("PM4r", [128, 32])
        PM4i = sb("PM4i", [128, 32])
        c1s = sb("c1s", [128, 32])
        s1s = sb("s1s", [128, 32])
        wcar = sb("wcar", [128, 2, 32])
        expA = sb("expA", [16, 1])
        dma(Fsel[:, :, :, :].rearrange("p a b c -> p (a b c)"), d_fsel[:, :], ["d_fsel"], ["Fsel"], cast=True)

        with ExitStack() as es2:
            def sb2(name, shape, dt=F32):
                return es2.enter_context(nc.sbuf_tensor("sb2_" + name, list(shape), dt))

            lamC = sb2("lamC", [128, 3, 32])
            bC = sb2("bC", [128, 2, 32, 16])
            cC = sb2("cC", [128, 2, 32, 16])
            kv2 = sb2("kv2", [128, 2, 16, 32])
            dma(lamC[:, :, :], d_lamC[:, :, :], ["d_lamC"], ["lamC"])
            dma(bC[:, :, :, :].rearrange("p a b c -> p (a b c)"), d_bC[:, :], ["d_bC"], ["bC"])
            dma(cC[:, :, :, :].rearrange("p a b c -> p (a b c)"), d_cC[:, :], ["d_cC"], ["cC"])
            dma(kv2[:, :, :, :].rearrange("p a b c -> p (a b c)"), d_kv2[:, :], ["d_kv2"], ["kv2"])
            S.add("dve", lambda e: e.memset(wcar[:, :, :], 0.0), [], ["wcar"])
            sm = {}

            def small(name):
                sm[name] = sb2("sm_" + name, [128, 32])
                return sm[name][:, :]
            lr, li, ldt = lamC[:, 0, :], lamC[:, 1, :], lamC[:, 2, :]
            dl, lrd, th, mg, kk, fr, sn, cs_, t0, t1_, fre, fim = [small(n) for n in
                ("dl", "lrd", "th", "mg", "kk", "fr", "sn", "cs", "t0", "t1", "fre", "fim")]
            K = lambda n: "sm_" + n
            act(dl, ldt, AF.Exp, ["lamC"], [K("dl")])
            tt(lrd, lr, dl, ALU.mult, ["lamC", K("dl")], [K("lrd")])
            tt(th, li, dl, ALU.mult, ["lamC", K("dl")], [K("th")])
            tsc(th, th, 1.0 / TWO_PI, None, ALU.mult, None, [K("th")], [K("th")])
            tsc(TH8[:, :], th, 8.0, None, ALU.mult, None, [K("th")], ["TH8"])
            act(mg, lrd, AF.Exp, [K("lrd")], [K("mg")])
            tsc(kk, th, MAGIC, MAGIC, ALU.add, ALU.subtract, [K("th")], [K("kk")])
            tt(fr, th, kk, ALU.subtract, [K("th"), K("kk")], [K("fr")])
            act(sn, fr, AF.Sin, [K("fr")], [K("sn")], scale=TWO_PI)
            act(kk, fr, AF.Abs, [K("fr")], [K("kk")])
            act(cs_, kk, AF.Sin, [K("kk"), "cf"], [K("cs")], bias=cf[:, 769:770], scale=-TWO_PI)
            tt(cs_, cs_, mg, ALU.mult, [K("cs"), K("mg")], [K("cs")])
            tt(sn, sn, mg, ALU.mult, [K("sn"), K("mg")], [K("sn")])
            tsc(cs_, cs_, -1.0, None, ALU.add, None, [K("cs")], [K("cs")])
            tt(mg, lr, lr, ALU.mult, ["lamC"], [K("mg")])
            tt(kk, li, li, ALU.mult, ["lamC"], [K("kk")])
            tt(mg, mg, kk, ALU.add, [K("mg"), K("kk")], [K("mg")])
            S.add("dve", lambda e: e.reciprocal(mg, mg), [K("mg")], [K("mg")])
            tt(t0, cs_, lr, ALU.mult, [K("cs"), "lamC"], [K("t0")])
            tt(t1_, sn, li, ALU.mult, [K("sn"), "lamC"], [K("t1")])
            tt(t0, t0, t1_, ALU.add, [K("t0"), K("t1")], [K("t0")])
            tt(fre, t0, mg, ALU.mult, [K("t0"), K("mg")], [K("fre")])
            tt(t0, sn, lr, ALU.mult, [K("sn"), "lamC"], [K("t0")])
            tt(t1_, cs_, li, ALU.mult, [K("cs"), "lamC"], [K("t1")])
            tt(t0, t0, t1_, ALU.subtract, [K("t0"), K("t1")], [K("t0")])
            tt(fim, t0, mg, ALU.mult, [K("t0"), K("mg")], [K("fim")])
            _tt, _tsc, _act, _cp, _mm, _stt = tt, tsc, act, cp, mm, stt
            Bbr = sb2("Bbr", [128, 32, 16])
            Bbi = sb2("Bbi", [128, 32, 16])
            tb = sb2("tb", [128, 32, 16])
            bc16 = lambda a: a.unsqueeze(2).broadcast_to([128, 32, 16])
            tt(Bbr[:, :, :], bc16(fre), bC[:, 0, :, :], ALU.mult, [K("fre"), "bC"], ["Bbr"])
            tt(tb[:, :, :], bc16(fim), bC[:, 1, :, :], ALU.mult, [K("fim"), "bC"], ["tb"])
            tt(Bbr[:, :, :], Bbr[:, :, :], tb[:, :, :], ALU.subtract, ["Bbr", "tb"], ["Bbr"])
            tt(Bbi[:, :, :], bc16(fre), bC[:, 1, :, :], ALU.mult, [K("fre"), "bC"], ["Bbi"])
            tt(tb[:, :, :], bc16(fim), bC[:, 0, :, :], ALU.mult, [K("fim"), "bC"], ["tb"])
            tt(Bbi[:, :, :], Bbi[:, :, :], tb[:, :, :], ALU.add, ["Bbi", "tb"], ["Bbi"])
            PWr = sb2("PWr", [128, 2, 16, 32])
            PWi = sb2("PWi", [128, 2, 16, 32])
            pa = sb2("pa", [128, 2, 16, 32])
            pk_ = sb2("pk", [128, 2, 16, 32])
            pf = sb2("pf", [128, 2, 16, 32])
            pm = sb2("pm", [128, 2, 16, 32])
            bck = lambda a: a.unsqueeze(1).unsqueeze(1).broadcast_to([128, 2, 16, 32])
            tt(pa[:, :, :, :], kv2[:, :, :, :], bck(th), ALU.mult, ["kv2", K("th")], ["pa"])
            tsc(pk_[:, :, :, :], pa[:, :, :, :], MAGIC, MAGIC, ALU.add, ALU.subtract, ["pa"], ["pk"])
            tt(pf[:, :, :, :], pa[:, :, :, :], pk_[:, :, :, :], ALU.subtract, ["pa", "pk"], ["pf"])
            tt(pm[:, :, :, :], kv2[:, :, :, :], bck(lrd), ALU.mult, ["kv2", K("lrd")], ["pm"])
            act(pm[:, :, :, :], pm[:, :, :, :], AF.Exp, ["pm"], ["pm"])
            act(PWi[:, :, :, :], pf[:, :, :, :], AF.Sin, ["pf"], ["PWi"], scale=TWO_PI)
            act(pk_[:, :, :, :], pf[:, :, :, :], AF.Abs, ["pf"], ["pk"])
            act(PWr[:, :, :, :], pk_[:, :, :, :], AF.Sin, ["pk", "cf"], ["PWr"], bias=cf[:, 769:770], scale=-TWO_PI)
            cp(c1s[:, :], PWr[:, 0, 15, :], ["PWr"], ["c1s"])
            cp(s1s[:, :], PWi[:, 0, 15, :], ["PWi"], ["s1s"])
            cp(R8[:, :], pm[:, 0, 15, :], ["pm"], ["R8"])
            tt(PWr[:, :, :, :], PWr[:, :, :, :], pm[:, :, :, :], ALU.mult, ["PWr", "pm"], ["PWr"])
            tt(PWi[:, :, :, :], PWi[:, :, :, :], pm[:, :, :, :], ALU.mult, ["PWi", "pm"], ["PWi"])
            cp(PM4r[:, :], PWr[:, 0, 3, :], ["PWr"], ["PM4r"])
            cp(PM4i[:, :], PWi[:, 0, 3, :], ["PWi"], ["PM4i"])


            def pw4(T, d, i0):
                return T[:, d, i0:i0 + 8, :].rearrange("p k g -> p g k").unsqueeze(3).broadcast_to([128, 32, 8, 16])
            bc8 = lambda a: a.unsqueeze(2).broadcast_to([128, 32, 8, 16])
            arr = [sb2("arr%d" % i, [128, 32, 8, 16]) for i in range(4)]
            tmpa = sb2("tmpa", [128, 32, 8, 16])

            def cmul(out_re, out_im, d, i0, xr, xi, xk, neg_im, okr, oki):
                tt(out_re, pw4(PWr, d, i0), bc8(xr), ALU.mult, ["PWr"] + xk, [okr])
                tt(tmpa[:, :, :, :], pw4(PWi, d, i0), bc8(xi), ALU.mult, ["PWi"] + xk, ["tmpa"])
                tt(out_re, out_re, tmpa[:, :, :, :], ALU.subtract, [okr, "tmpa"], [okr])
                tt(out_im, pw4(PWr, d, i0), bc8(xi), ALU.mult, ["PWr"] + xk, [oki])
                tt(tmpa[:, :, :, :], pw4(PWi, d, i0), bc8(xr), ALU.mult, ["PWi"] + xk, ["tmpa"])
                if neg_im:
                    stt(out_im, out_im, -1.0, tmpa[:, :, :, :], ALU.mult, ALU.subtract, [oki, "tmpa"], [oki])
                else:
                    tt(out_im, out_im, tmpa[:, :, :, :], ALU.add, [oki, "tmpa"], [oki])
            A4 = [a[:, :, :, :] for a in arr]
            cmul(A4[0], A4[1], 1, 8, Bbr[:, :, :], Bbi[:, :, :], ["Bbr", "Bbi"], False, "arr0", "arr1")
            cmul(A4[2], A4[3], 0, 7, cC[:, 0, :, :], cC[:, 1, :, :], ["cC"], True, "arr2", "arr3")
            maskT = cf[:, 772:900]
            ident_f = cf[:, 128:256]
            for g in range(64):
                pi_, hf = g // 2, g % 2
                rows = slice(64 * hf, 64 * hf + 64)
                ps, pk = ps_next()
                mm(ps[:, 0:128], arr[0][rows, pi_, :, :].rearrange("p a b -> p (a b)"),
                   arr[2][rows, pi_, :, :].rearrange("p a b -> p (a b)"), True, False, ["arr0", "arr2"], [pk])
                mm(ps[:, 0:128], arr[1][rows, pi_, :, :].rearrange("p a b -> p (a b)"),
                   arr[3][rows, pi_, :, :].rearrange("p a b -> p (a b)"), False, True, ["arr1", "arr3"], [pk])
                tt(Toep[:, g, :], ps[:, 0:128], maskT, ALU.mult, [pk, "cf"], ["Toep"])
            cmul(A4[0], A4[1], 1, 1, Bbr[:, :, :], Bbi[:, :, :], ["Bbr", "Bbi"], False, "arr0", "arr1")
            for g0 in range(0, 64, 8):
                for part, (src, dstW, sk) in enumerate(((arr[0], Wre, "arr0"), (arr[1], Wim, "arr1"))):
                    pss = [ps_next(), ps_next()]
                    for gi in range(8):
                        g = g0 + gi
                        pi_, hf = g // 2, g % 2
                        rows = slice(64 * hf, 64 * hf + 64)
                        ps, pk = pss[hf]
                        mm(ps[:, (gi // 2) * 64:(gi // 2 + 1) * 64], src[rows, pi_, :, :].rearrange("p a b -> p (a b)"),
                           ident_f[rows, 64 * hf:64 * hf + 64], True, True, [sk, "cf"], [pk])
                    for hf in range(2):
                        ps, pk = pss[hf]
                        act(dstW[:, g0:g0 + 8, :].rearrange("p (a b) c -> p a b c", b=2)[:, :, hf, :],
                            ps[:, 0:256].rearrange("p (a c) -> p a c", c=64), AF.Identity, [pk],
                            ["Wre" if part == 0 else "Wim"])
            cmul(A4[2], A4[3], 0, 8, cC[:, 0, :, :], cC[:, 1, :, :], ["cC"], True, "arr2", "arr3")
            cp(CoR[:, :, :], arr[2][:, :, :, :].rearrange("p g a b -> p g (a b)"), ["arr2"], ["CoR"])
            cp(CoI[:, :, :], arr[3][:, :, :, :].rearrange("p g a b -> p g (a b)"), ["arr3"], ["CoI"])
            _act(expA[:, :], p16[:, 1:2], AF.Exp, ["p16"], ["expA"])
            S.barrier()
            tt, tsc, act, cp, mm, stt = _tt, _tsc, _act, _cp, _mm, _stt

        W = WP
        xT = sb("xT", [128, 8, W])
        nT = sb("nT", [128, 8, W], BF16)
        class Rot:
            def __init__(self, name, shape, n, dt=F32):
                self.t = [(sb("%s_r%d" % (name, i), shape, dt), "%s_r%d" % (name, i)) for i in range(n)]
                self.i = 0

            def nxt(self):
                self.i += 1
                return self.t[self.i % len(self.t)]
        SQB = Rot("sqb", [128, W], 3, BF16)
        CA = Rot("cacc", [128, W], 2)
        CG = Rot("csig", [128, W], 2)
        rstd = sb("rstd", [128, W])
        wbuf = [sb("wbuf%d" % i, [128, 8, 128], BF16) for i in range(8)]
        gza = sb("gza", [128, 8, W], BF16)
        gzb = sb("gzb", [128, 8, W], BF16)
        xbc = sb("xbc", [128, 12, 3 + W])
        xs = sb("xs", [128, 8, W], BF16)
        BTb = sb("BTb", [128, 2, W], BF16)
        CTb = sb("CTb", [128, 2, W], BF16)
        uT = sb("uT", [128, 8, W], BF16)
        dtr = sb("dtr", [16, W])
        dtT = sb("dtT", [16, W])
        lndt = sb("lndt", [16, W])
        AcsT = sb("AcsT", [16, W])
        negA = sb("negA", [16, 128])
        negAq = sb("negAq", [16, 4, 128])
        bdA = sb("bdA", [16, 4, 128])
        cbt = sb("cbt", [128, 2, 128])
        eAqs = [sb("eAq%d" % i, [128, 4, 128]) for i in range(2)]
        decqs = [sb("decq%d" % i, [128, 4, 128]) for i in range(2)]
        MTqs = [sb("MTq%d" % i, [128, 4, 128], BF16) for i in range(2)]
        CTss = [sb("CTs%d" % i, [128, 4, 128], BF16) for i in range(2)]
        wendt = sb("wendt", [128, 16])
        cdt = sb("cdt", [128, 16])
        cds = sb("cds", [128, 16, NSEQ_S])
        NCH = WP // 8
        Uall = sb("Uall", [128, 8, 8, NCH], BF16)
        Yall = sb("Yall", [128, 8, 8, NCH], BF16)
        _nflat = nT[:, :, :].rearrange("p a b -> p (a b)")
        xtok = _nflat[:, 0:1024]
        xw = _nflat[:, 1024:2048]
        Btok = sb("Btok", [128, 2, 128], BF16)
        Btokm = sb("Btokm", [64, 2, 128], BF16)
        hT = sb("hT", [128, 1024])
        hTb = sb("hTb", [128, 1024], BF16)
        htmp = sb("htmp", [128, 1024])
        cstage = htmp[:, 0:12 * NSEQ_S * 3].rearrange("p (j t) -> p j t", t=NSEQ_S * 3)
        cstage2 = cstage
        yoff = Yall[:, :, :, :].rearrange("p a b c -> p (a b c)")[:, 0:2 * 8 * WS].bitcast(F32).rearrange("p (a b) -> p a b", b=WS)
        dtmp = Yall[0:64, :, :, :].rearrange("p a b c -> p (a b c)")[:, 1024:1536].bitcast(F32).rearrange("p (a b) -> p a b", b=64)
        yT = sb("yT", [128, 8, W])
        yab = sb("yab", [128, 16, W], BF16)
        gbf = xs
        pTb = sb("pTb", [128, 2, W], BF16)
        NCH = WP // 8
        def dbl(name, shape, dt=F32):
            return [sb("%s_%d" % (name, i), shape, dt) for i in range(2)]
        l_ang = [sb("l_ang", [128, 4, NCH + 1])] * 2
        l_kk = [sb("l_kk", [128, 4, NCH + 1])] * 2
        l_fr = [sb("l_fr", [128, 4, NCH + 1])] * 2
        l_cos = dbl("l_cos", [128, 4, NCH + 1])
        l_sin = dbl("l_sin", [128, 4, NCH + 1])
        l_t = [[sb("l_t%d" % i, [128, 4, NCH])] * 2 for i in range(4)]
        l_pq = [sb("l_pq%d" % i, [128, 4, NCH]) for i in range(2)]
        l_bre = dbl("l_bre", [128, 4, NCH])
        l_bim = dbl("l_bim", [128, 4, NCH])
        l_wr = dbl("l_wr", [128, 4, NCH + 1])
        l_wi = dbl("l_wi", [128, 4, NCH + 1])
        xre_all = sb("xre_all", [128, 32, NCH], BF16)
        xim_all = sb("xim_all", [128, 32, NCH], BF16)
        xi_t = dbl("xi_t", [128, 2, 4, NSEQ_S])
        sf_t = dbl("sf_t", [128, 2, 4, NSEQ_S])
        sp_t = dbl("sp_t", [128, 2, 4])

        def pv(col):
            return pvec[:, col:col + 1]
        glubh = sb("glubh", [128, 8])
        tsc(glubh[:, :], pvec[:, PV_GB:PV_GB + 8], 0.5, None, ALU.mult, None, ["pvec"], ["glubh"])

        S.add("dve", lambda e: e.memset(hT[:, :], 0.0), [], ["hT"])
        S.add("dve", lambda e: e.memset(hTb[:, :], 0.0), [], ["hTb"])
        S.add("dve", lambda e: e.memset(xbc[:, :, 0:3], 0.0), [], ["xbc"])

        out_dmas = []

        def rms_stats(src_tile, src_key, Wb, mean_n, tiles):
            ps, pk = ps_next()
            for i, j in enumerate(tiles):
                sqb, sqk = SQB.nxt()
                act(sqb[:, 0:Wb], src_tile[:, j, 0:Wb], AF.Square, [src_key % j if "%d" in src_key else src_key], [sqk])
                mm(ps[:, 0:Wb], ones_b, sqb[:, 0:Wb], i == 0, i == len(tiles) - 1, ["cbf", sqk], [pk])
            act(rstd[:, 0:Wb], ps[:, 0:Wb], AF.Ln, [pk, "cf"], ["rstd"], bias=cf[:, 770:771], scale=1.0 / mean_n)
            act(rstd[:, 0:Wb], rstd[:, 0:Wb], AF.Exp, ["rstd"], ["rstd"], scale=-0.5)

        first_pass = [True]

        def stream_matmul(wname, dsrc, nblk, nk, bufs, bkey, rhs_fn, rhs_keys, Wb, epilogue, nsplit=1, psf=None, idx=None):
            nb = len(bufs)
            nld = nblk * nsplit
            dscr_ = scr[wname]

            def load(i):
                bk_ = "%s%d" % (bkey, i % nb)
                di = idx(i) if idx is not None else i
                if first_pass[0]:
                    dma(bufs[i % nb][:, 0:nk, :], dsrc[di], ["dw"], [bk_], cast=True)
                    dma(dscr_[di], bufs[i % nb][:, 0:nk, :], [bk_], ["scr_%s_%d" % (wname, di)])
                else:
                    dma(bufs[i % nb][:, 0:nk, :], dscr_[di], ["scr_%s_%d" % (wname, di)], [bk_])
            for i in range(min(nb - 1, nld)):
                load(i)
            for j in range(nblk):
                ps, pk = psf(j) if psf is not None else ps_next()
                for h in range(nsplit):
                    i = j * nsplit + h
                    if i + nb - 1 < nld:
                        load(i + nb - 1)
                    for k in range(nk):
                        mm(ps[:, 0:Wb], bufs[i % nb][:, k, :], rhs_fn(h * nk + k), (h == 0 and k == 0),
                           (h == nsplit - 1 and k == nk - 1), ["%s%d" % (bkey, i % nb)] + rhs_keys, [pk])
                if epilogue is not None:
                    epilogue(j, ps, pk)

        def do_block(c0, Wb, sample, bidx, last_prompt):
            L = 64 if sample else 128
            nchunk = 1 if sample else Wb // 128
            for k in range(8):
                dma(xT[:, k, 0:Wb], d_xT[k * 128:(k + 1) * 128, c0:c0 + Wb], ["d_xT"], ["xT%d" % k])
            dma(pTb[:, :, 0:Wb], d_pT.rearrange("(k p) c -> p k c", p=128)[:, :, c0:c0 + Wb], ["d_pT"], ["pTb"], cast=True)
            for k in range(8):
                tsc(nT[:, k, 0:Wb], xT[:, k, 0:Wb], pv(PV_GIN + k), None, ALU.mult, None, ["xT%d" % k, "pvec"], ["nT"])
            rms_stats(xT, "xT%d", Wb, float(D), list(range(8)))
            if sample:
                dma(cstage[:, :, :], d_chist.rearrange("p j q t -> p j (q t)"), ["d_chist"], ["htmp"])
                cp(xbc[:, :, 0:NSEQ_S * 7].rearrange("p j (q t) -> p j q t", t=7)[:, :, :, 0:3],
                   cstage[:, :, :].rearrange("p j (q t) -> p j q t", t=3), ["htmp"], ["xbc"])

            def xbc_new(j):
                if sample:
                    return xbc[:, j, 0:NSEQ_S * 7].rearrange("p (q t) -> p q t", t=7)[:, :, 3:7]
                return xbc[:, j, 3:3 + Wb]

            def ep_in(j, ps, pk):
                rs = rstd[:, 0:Wb]
                if j < 8 or 21 <= j < 29:
                    cacc, cak = CA.nxt()
                    tt(cacc[:, 0:Wb], ps[:, 0:Wb], rs, ALU.mult, [pk, "rstd"], [cak])
                    if j < 8:
                        act(gza[:, j, 0:Wb], cacc[:, 0:Wb], AF.Silu, [cak], ["gza"])
                    else:
                        act(gzb[:, j - 21, 0:Wb], cacc[:, 0:Wb], AF.Silu, [cak], ["gzb"])
                elif j < 20:
                    if sample:
                        tt(xbc_new(j - 8), ps[:, 0:Wb].rearrange("p (q t) -> p q t", t=TS),
                           rs.rearrange("p (q t) -> p q t", t=TS), ALU.mult, [pk, "rstd"], ["xbc"])
                    else:
                        tt(xbc_new(j - 8), ps[:, 0:Wb], rs, ALU.mult, [pk, "rstd"], ["xbc"])
                elif j == 20:
                    tt(dtr[:, 0:Wb], ps[0:16, 0:Wb], rstd[0:16, 0:Wb], ALU.mult, [pk, "rstd"], ["dtr"])
                else:
                    tt(uT[:, j - 29, 0:Wb], ps[:, 0:Wb], rs, ALU.mult, [pk, "rstd"], ["uT"])

            stream_matmul("win", d_win, NBLK_IN, 8, wbuf, "wbuf", lambda k: nT[:, k, 0:Wb], ["nT"], Wb, ep_in)

            if sample:
                v = xbc[:, :, 0:NSEQ_S * 7].rearrange("p j (q t) -> p j q t", t=7)[:, :, :, 4:7]
                cp(cstage2[:, :, :].rearrange("p j (q t) -> p j q t", t=3), v, ["xbc"], ["htmp"])
                out_dmas.append(dma(o_convs.rearrange("p j q t -> p j (q t)"), cstage2[:, :, :], ["htmp"], ["o_convs"]))
            elif last_prompt:
                cp(cstage2[:, :, 0:3], xbc[:, :, Wb:Wb + 3], ["xbc"], ["htmp"])
                out_dmas.append(dma(o_convp[:, :, :], cstage2[:, :, 0:3], ["htmp"], ["o_convp"]))

            nch = NSEQ_S if sample else Wb // 8
            svals = list(range(4, 8)) if sample else list(range(8))
            nsv = 4 if sample else 8
            ncl = 8 * nch
            for gl in range(8):
                hb = gl // 4
                hh = slice(64 * hb, 64 * hb + 64)
                bk = 2 * hb + (gl % 4) // 2
                o_ = psb[bk][:, (gl % 2) * ncl:(gl % 2 + 1) * ncl].rearrange("p (j c) -> p j c", c=nch)
                for si, sv in enumerate(svals):
                    rhs = uT[hh, :, 0:Wb].rearrange("p j (c s) -> p j c s", s=nsv)[:, :, :, sv % nsv]
                    mm(o_, Fsel[hh, gl % 4, sv, :], rhs, si == 0, si == len(svals) - 1, ["Fsel", "uT"], ["psb%d" % bk])
            for bk in range(4):
                act(Uall[:, 2 * bk:2 * bk + 2, :, 0:nch],
                    psb[bk][:, 0:2 * ncl].rearrange("p (g j c) -> p g j c", g=2, c=nch), AF.Identity, ["psb%d" % bk], ["Uall"])
            psS, psSk = [], []
            for j in range(8):
                bk = 4 + j // 2
                psS.append(psb[bk][:, (j % 2) * 256:(j % 2) * 256 + 8 * nch].rearrange("p (a b c) -> p a b c", b=2, c=nch))
                psSk.append("psb%d" % bk)
                for gl in range(8):
                    g = 8 * j + gl
                    pr, hf = gl // 2, gl % 2
                    rows = slice(64 * hf, 64 * hf + 64)
                    mm(psS[j][rows, pr, 0, :], Wre[:, g, :], Uall[:, gl, j, 0:nch], True, True, ["Wre", "Uall"], [psSk[j]])
                    mm(psS[j][rows, pr, 1, :], Wim[:, g, :], Uall[:, gl, j, 0:nch], True, True, ["Wim", "Uall"], [psSk[j]])
            ps_banks[0] = [0, 1, 2] if sample else [0, 1, 2, 3]
            for j in range(12):
                cacc, cak = CA.nxt()

                def tap(k, j=j):
                    if sample:
                        return xbc[:, j, 0:NSEQ_S * 7].rearrange("p (q t) -> p q t", t=7)[:, :, k:k + 4]
                    return xbc[:, j, k:k + Wb]
                a3 = cacc[:, 0:Wb].rearrange("p (q t) -> p q t", t=TS) if sample else cacc[:, 0:Wb]
                tsc(a3, tap(0), pv(PV_CW + j * 4), None, ALU.mult, None, ["xbc", "pvec"], [cak])
                for k in range(1, 4):
                    stt(a3, tap(k), pv(PV_CW + j * 4 + k), a3, ALU.mult, ALU.add, ["xbc", "pvec", cak], [cak])
                if j < 8:
                    dst, dk = xs[:, j, 0:Wb], "xs"
                elif j < 10:
                    dst, dk = BTb[:, j - 8, 0:Wb], "BTb"
                else:
                    dst, dk = CTb[:, j - 10, 0:Wb], "CTb"
                act(dst, cacc[:, 0:Wb], AF.Silu, [cak, "pvec"], [dk], bias=pv(PV_CB + j))
            if not sample:
                cp(xbc[:, :, 0:3], xbc[:, :, Wb:Wb + 3], ["xbc"], ["xbc"])

            nch = NSEQ_S if sample else Wb // 8
            cbase = 0.0 if sample else float(c0 // 8)
            iota33 = cf[:, 900:933]
            def s5_level2(j):
                par = j % 2
                kx = lambda n: n if n in ("l_ang", "l_kk", "l_fr", "l_t0", "l_t1", "l_t2", "l_t3") else "%s_%d" % (n, par)
                prs = slice(4 * j, 4 * j + 4)
                Sre, Sim = psS[j][:, :, 0, :], psS[j][:, :, 1, :]
                bre, bim = l_bre[par][:, :, 0:nch], l_bim[par][:, :, 0:nch]
                t_ = [l_t[i][par][:, :, 0:nch] for i in range(4)]
                tk = [kx("l_t%d" % i) for i in range(4)]
                if sample:
                    cM = c1s[:, prs].unsqueeze(2).broadcast_to([128, 4, nch])
                    sM = s1s[:, prs].unsqueeze(2).broadcast_to([128, 4, nch])
                    tabk = ["c1s", "s1s"]
                else:
                    ang, kk_, fr_ = l_ang[par][:, :, 0:nch + 1], l_kk[par][:, :, 0:nch + 1], l_fr[par][:, :, 0:nch + 1]
                    cT, sT = l_cos[par][:, :, 0:nch + 1], l_sin[par][:, :, 0:nch + 1]
                    stt(ang, iota33[:, 0:nch + 1].unsqueeze(1).broadcast_to([128, 4, nch + 1]), cbase,
                        TH8[:, prs].unsqueeze(2).broadcast_to([128, 4, nch + 1]), ALU.add, ALU.mult, ["cf", "TH8"], [kx("l_ang")])
                    tsc(kk_, ang, MAGIC, MAGIC, ALU.add, ALU.subtract, [kx("l_ang")], [kx("l_kk")])
                    tt(fr_, ang, kk_, ALU.subtract, [kx("l_ang"), kx("l_kk")], [kx("l_fr")])
                    act(sT, fr_, AF.Sin, [kx("l_fr")], [kx("l_sin")], scale=TWO_PI)
                    act(kk_, fr_, AF.Abs, [kx("l_fr")], [kx("l_kk")])
                    act(cT, kk_, AF.Sin, [kx("l_kk"), "cf"], [kx("l_cos")], bias=cf[:, 769:770], scale=-TWO_PI)
                    cM, sM = cT[:, :, 1:nch + 1], sT[:, :, 1:nch + 1]
                    tabk = [kx("l_cos"), kx("l_sin")]
                tt(t_[0], cM, Sre, ALU.mult, tabk + [psSk[j]], [tk[0]])
                tt(t_[1], sM, Sim, ALU.mult, tabk + [psSk[j]], [tk[1]])
                tt(bre, t_[0], t_[1], ALU.add, [tk[0], tk[1]], [kx("l_bre")])
                tt(t_[2], cM, Sim, ALU.mult, tabk + [psSk[j]], [tk[2]])
                tt(t_[3], sM, Sre, ALU.mult, tabk + [psSk[j]], [tk[3]])
                tt(bim, t_[2], t_[3], ALU.subtract, [tk[2], tk[3]], [kx("l_bim")])
                wr, wi = l_wr[par], l_wi[par]
                xre, xim = xre_all[:, prs, 0:nch], xim_all[:, prs, 0:nch]
                if sample:
                    xi, xk_ = xi_t[par], kx("xi_t")
                    sf, sk_ = sf_t[par], kx("sf_t")
                    dma(xi[:, :, :, :], d_winitC.rearrange("p (a b c) -> p a b c", a=2, c=NSEQ_S)[:, :, 4 * j:4 * j + 4, :],
                        ["d_winitC"], [xk_])
                    bq = lambda a_: a_[:, prs].unsqueeze(2).broadcast_to([128, 4, NSEQ_S])
                    tt(sf[:, 0, :, :], xi[:, 1, :, :], bq(PM4i), ALU.mult, [xk_, "PM4i"], [sk_])
                    tt(sf[:, 1, :, :], xi[:, 0, :, :], bq(PM4i), ALU.mult, [xk_, "PM4i"], [sk_])
                    tt(xi[:, 0, :, :], xi[:, 0, :, :], bq(PM4r), ALU.mult, [xk_, "PM4r"], [xk_])
                    tt(xi[:, 0, :, :], xi[:, 0, :, :], sf[:, 0, :, :], ALU.subtract, [xk_, sk_], [xk_])
                    tt(xi[:, 1, :, :], xi[:, 1, :, :], bq(PM4r), ALU.mult, [xk_, "PM4r"], [xk_])
                    tt(xi[:, 1, :, :], xi[:, 1, :, :], sf[:, 1, :, :], ALU.add, [xk_, sk_], [xk_])
                    r8b = R8[:, prs].unsqueeze(2).broadcast_to([128, 4, nch])
                    tt(t_[0], xi[:, 0, :, :], r8b, ALU.mult, [xk_, "R8"], [tk[0]])
                    tt(wr[:, :, 0:nch], t_[0], bre, ALU.add, [tk[0], kx("l_bre")], [kx("l_wr")])
                    tt(t_[1], xi[:, 1, :, :], r8b, ALU.mult, [xk_, "R8"], [tk[1]])
                    tt(wi[:, :, 0:nch], t_[1], bim, ALU.add, [tk[1], kx("l_bim")], [kx("l_wi")])
                    cp(xre, xi[:, 0, :, :], [xk_], ["xre_all"])
                    cp(xim, xi[:, 1, :, :], [xk_], ["xim_all"])
                    tt(t_[2], cM, wr[:, :, 0:nch], ALU.mult, tabk + [kx("l_wr")], [tk[2]])
                    tt(t_[3], sM, wi[:, :, 0:nch], ALU.mult, tabk + [kx("l_wi")], [tk[3]])
                    tt(sf[:, 0, :, :], t_[2], t_[3], ALU.subtract, [tk[2], tk[3]], [sk_])
                    tt(t_[2], sM, wr[:, :, 0:nch], ALU.mult, tabk + [kx("l_wr")], [tk[2]])
                    tt(t_[3], cM, wi[:, :, 0:nch], ALU.mult, tabk + [kx("l_wi")], [tk[3]])
                    tt(sf[:, 1, :, :], t_[2], t_[3], ALU.add, [tk[2], tk[3]], [sk_])
                    out_dmas.append(dma(o_s5s.rearrange("p (a b c) -> p a b c", a=2, c=NSEQ_S)[:, :, 4 * j:4 * j + 4, :],
                                        sf[:, :, :, :], [sk_], ["o_s5s"]))
                else:
                    cp(wr[:, :, 0], wcar[:, 0, prs], ["wcar"], [kx("l_wr")], eng="pool")
                    cp(wi[:, :, 0], wcar[:, 1, prs], ["wcar"], [kx("l_wi")], eng="pool")
                    for pr in range(4):
                        pi_ = 4 * j + pr
                        S.add("dve", lambda e, pr=pr, pi_=pi_, wr=wr, bre=bre: e.tensor_tensor_scan(
                            wr[:, pr, 1:nch + 1], R8[:, pi_:pi_ + 1].broadcast_to([128, nch]), bre[:, pr, :], wr[:, pr, 0:1],
                            ALU.mult, ALU.add), ["R8", kx("l_bre"), kx("l_wr")], [kx("l_wr")])
                        S.add("dve", lambda e, pr=pr, pi_=pi_, wi=wi, bim=bim: e.tensor_tensor_scan(
                            wi[:, pr, 1:nch + 1], R8[:, pi_:pi_ + 1].broadcast_to([128, nch]), bim[:, pr, :], wi[:, pr, 0:1],
                            ALU.mult, ALU.add), ["R8", kx("l_bim"), kx("l_wi")], [kx("l_wi")])
                    cp(wcar[:, 0, prs], wr[:, :, nch], [kx("l_wr")], ["wcar"], eng="pool")
                    cp(wcar[:, 1, prs], wi[:, :, nch], [kx("l_wi")], ["wcar"], eng="pool")
                    cD, sD = cT[:, :, 0:nch], sT[:, :, 0:nch]
                    pq = [l_pq[i][:, :, 0:nch] for i in range(2)]
                    ptt(pq[0], cD, wr[:, :, 0:nch], ALU.mult, tabk + [kx("l_wr")], ["l_pq0"])
                    ptt(pq[1], sD, wi[:, :, 0:nch], ALU.mult, tabk + [kx("l_wi")], ["l_pq1"])
                    ptt(xre, pq[0], pq[1], ALU.subtract, ["l_pq0", "l_pq1"], ["xre_all"])
                    ptt(pq[0], sD, wr[:, :, 0:nch], ALU.mult, tabk + [kx("l_wr")], ["l_pq0"])
                    ptt(pq[1], cD, wi[:, :, 0:nch], ALU.mult, tabk + [kx("l_wi")], ["l_pq1"])
                    ptt(xim, pq[0], pq[1], ALU.add, ["l_pq0", "l_pq1"], ["xim_all"])
                    if last_prompt:
                        tt(bre[:, :, 0], cT[:, :, nch], wr[:, :, nch], ALU.mult, tabk + [kx("l_wr")], [kx("l_bre")])
                        tt(bim[:, :, 0], sT[:, :, nch], wi[:, :, nch], ALU.mult, tabk + [kx("l_wi")], [kx("l_bim")])
                        spt, spk = sp_t[par], kx("sp_t")
                        tt(spt[:, 0, :], bre[:, :, 0], bim[:, :, 0], ALU.subtract, [kx("l_bre"), kx("l_bim")], [spk])
                        tt(bre[:, :, 0], sT[:, :, nch], wr[:, :, nch], ALU.mult, tabk + [kx("l_wr")], [kx("l_bre")])
                        tt(bim[:, :, 0], cT[:, :, nch], wi[:, :, nch], ALU.mult, tabk + [kx("l_wi")], [kx("l_bim")])
                        tt(spt[:, 1, :], bre[:, :, 0], bim[:, :, 0], ALU.add, [kx("l_bre"), kx("l_bim")], [spk])
                        out_dmas.append(dma(o_s5p.rearrange("p (a b) -> p a b", a=2)[:, :, 4 * j:4 * j + 4], spt[:, :, :], [spk], ["o_s5p"]))
            jvals = list(range(4, 8)) if sample else list(range(8))

            def s5_back_Y(h):
                for j in range(4 * h, 4 * h + 4):
                    for gl in range(8):
                        g = 8 * j + gl
                        pr, hf = gl // 2, gl % 2
                        rows = slice(64 * hf, 64 * hf + 64)
                        bk = 4 + 2 * h + hf
                        o_ = psb[bk][:, (j % 4) * 4 * nch + pr * nch:(j % 4) * 4 * nch + (pr + 1) * nch]
                        mm(o_, Toep[:, g, :], Uall[:, gl, j, 0:nch], True, False, ["Toep", "Uall"], ["psb%d" % bk])
                        mm(o_, CoR[rows, g // 2, :], xre_all[rows, g // 2, 0:nch], False, False, ["CoR", "xre_all"], ["psb%d" % bk])
                        mm(o_, CoI[rows, g // 2, :], xim_all[rows, g // 2, 0:nch], False, True, ["CoI", "xim_all"], ["psb%d" % bk])
                for hf in range(2):
                    bk = 4 + 2 * h + hf
                    act(Yall[:, 4 * h:4 * h + 4, :, 0:nch].rearrange("p j (a b) c -> p j a b c", b=2)[:, :, :, hf, :],
                        psb[bk][:, 0:16 * nch].rearrange("p (j a c) -> p j a c", a=4, c=nch), AF.Identity, ["psb%d" % bk], ["Yall"])

            def s5_back_T(h):
                for jj in jvals:
                    hb = jj // 4
                    hh = slice(64 * hb, 64 * hb + 64)
                    bk = 4 + 2 * h + hb
                    o_ = psb[bk][:, (jj % 4) * 4 * nch:(jj % 4 + 1) * 4 * nch].rearrange("p (j c) -> p j c", c=nch)
                    for gl in range(8):
                        mm(o_, Fsel[hh, jj % 4, gl, :], Yall[hh, 4 * h:4 * h + 4, gl, 0:nch], gl == 0, gl == 7,
                           ["Fsel", "Yall"], ["psb%d" % bk])

            def s5_back_evac(h):
                for hb in sorted(set(jj // 4 for jj in jvals)):
                    bk = 4 + 2 * h + hb
                    if sample:
                        dst = yT[:, 4 * h:4 * h + 4, 0:Wb].rearrange("p j (c s) -> p j c s", s=TS)
                    else:
                        dst = yT[:, 4 * h:4 * h + 4, 0:Wb].rearrange("p j (c s) -> p j c s", s=8)[:, :, :, 4 * hb:4 * hb + 4]
                    src = psb[bk][:, 0:16 * nch].rearrange("p (s j c) -> p j c s", s=4, c=nch)
                    act(dst, src, AF.Identity, ["psb%d" % bk], ["yT"])
            s5_back_todo = [lambda: s5_back_Y(0), lambda: s5_back_T(0), lambda: s5_back_Y(1), lambda: s5_back_T(1)]
            s5_todo = list(range(8))
            act(dtT[:, 0:Wb], dtr[:, 0:Wb], AF.Exp, ["dtr", "p16"], ["dtT"], bias=p16[:, 0:1])
            act(dtT[:, 0:Wb], dtT[:, 0:Wb], AF.Ln, ["dtT", "cf"], ["dtT"], bias=cf[0:16, 771:772])
            act(lndt[:, 0:Wb], dtT[:, 0:Wb], AF.Ln, ["dtT"], ["lndt"])
            tsc(dtr[:, 0:Wb], dtT[:, 0:Wb], expA[:, 0:1], -1.0, ALU.mult, ALU.mult, ["dtT", "expA"], ["dtr"])
            rst = reset_s if sample else reset_p[:, 0:Wb]
            S.add("dve", lambda e: e.tensor_tensor_scan(AcsT[:, 0:Wb], rst, dtr[:, 0:Wb], 0.0, ALU.mult, ALU.add),
                  ["c16", "dtr"], ["AcsT"])
            maskneg = maskneg_s if sample else maskneg_p

            for ch in range(nchunk):
                cc = ch * 128
                cs_ = slice(cc, cc + L)
                ps, pk = ps_next()
                for gr in range(2):
                    mm(ps[0:L, gr * 128:gr * 128 + L], BTb[:, gr, cs_], CTb[:, gr, cs_], True, True, ["BTb", "CTb"], [pk])
                act(cbt[0:L, :, 0:L], ps[0:L, 0:256].rearrange("p (a b) -> p a b", b=128)[:, :, 0:L], AF.Identity, [pk], ["cbt"])
                tt(negA[:, 0:L], lndt[:, cs_], AcsT[:, cs_], ALU.subtract, ["lndt", "AcsT"], ["negA"])
                tt(negAq[:, :, 0:L], negA[:, 0:L].unsqueeze(1).broadcast_to([16, 4, L]),
                   quadmask.unsqueeze(2).broadcast_to([16, 4, L]), ALU.mult, ["negA", "c16"], ["negAq"])
                tt(bdA[:, :, 0:L], AcsT[:, cs_].unsqueeze(1).broadcast_to([16, 4, L]), blockind[:, :, 0:L], ALU.mult,
                   ["AcsT", "cf"], ["bdA"])
                ps, pk = ps_next()
                psv = ps[:, :].bitcast(BF16)
                for j in range(8):
                    tr(psv[0:L, j * 128:(j + 1) * 128], xs[:, j, cs_], ident_b, ["xs", "cbf"], [pk])
                act(xtok[0:L, :], psv[0:L, :], AF.Identity, [pk], ["nT"])
                ps, pk = ps_next()
                psv = ps[:, :].bitcast(BF16)
                for gr in range(2):
                    tr(psv[0:L, gr * 128:(gr + 1) * 128], BTb[:, gr, cs_], ident_b, ["BTb", "cbf"], [pk])
                act(Btok[0:L, :, :], psv[0:L, 0:256].rearrange("p (a b) -> p a b", b=128), AF.Identity, [pk], ["Btok"])

                yps = []
                ypcur = [None, None]

                def bufs_(q):
                    return (eAqs[q % 2], decqs[q % 2], MTqs[q % 2], CTss[q % 2],
                            "eAq%d" % (q % 2), "decq%d" % (q % 2), "MTq%d" % (q % 2), "CTs%d" % (q % 2))

                def s1(q):
                    eAq, decq, MTq, CTs, kE, kD, kM, kC = bufs_(q)
                    ps, pk = ps_next()
                    ps3 = ps[:, 0:4 * L].rearrange("p (a b) -> p a b", b=L)
                    mm(ps3, onesq[:, q, :], bdA[:, :, 0:L], True, True, ["c16", "bdA"], [pk])
                    act(eAq[:, :, 0:L], ps[:, 0:4 * L].rearrange("p (a b) -> p a b", b=L), AF.Exp, [pk], [kE])
                    ps, pk = ps_next()
                    ps3 = ps[0:L, 0:4 * L].rearrange("p (a b) -> p a b", b=L)
                    mm(ps3, negAq[:, q, 0:L], blockind[:, :, 0:L], True, False, ["negAq", "cf"], [pk])
                    mm(ps3, onesq[:, q, 0:L], bdA[:, :, 0:L], False, False, ["c16", "bdA"], [pk])
                    mm(ps3, ident_b[0:L, 0:L], maskneg[0:L, 0:4 * L].rearrange("p (a b) -> p a b", b=L), False, True, ["cbf"], [pk])
                    act(decq[0:L, :, 0:L], ps[0:L, 0:4 * L].rearrange("p (a b) -> p a b", b=L), AF.Exp, [pk], [kD])

                def s2(q):
                    gr = q // 2
                    eAq, decq, MTq, CTs, kE, kD, kM, kC = bufs_(q)
                    tt(MTq[0:L, :, 0:L], decq[0:L, :, 0:L], cbt[0:L, gr:gr + 1, 0:L].broadcast_to([L, 4, L]), ALU.mult,
                       [kD, "cbt"], [kM])
                    tt(CTs[:, :, 0:L], eAq[:, :, 0:L], CTb[:, gr:gr + 1, cs_].broadcast_to([128, 4, L]), ALU.mult,
                       [kE, "CTb"], [kC])
                    if sample:
                        tt(dtmp[:, :, :], decq[0:L, :, 0:L], lastmask_s.unsqueeze(1).broadcast_to([L, 4, L]), ALU.mult,
                           [kD, "cs64"], ["Yall"])
                        S.add("dve", lambda e, q=q: e.tensor_reduce(wendt[0:L, 4 * q:4 * q + 4], dtmp[:, :, :], AX.X, ALU.add),
                              ["Yall"], ["wendt"])
                        cp(cds[:, 4 * q:4 * q + 4, :], eAq[:, :, 0:L].rearrange("p a (q t) -> p a q t", t=TS)[:, :, :, TS - 1],
                           [kE], ["cds"])
                        cp(yab[:, q * 4:(q + 1) * 4, 0:L], CTs[:, :, 0:L], [kC], ["yab"])
                    else:
                        cp(wendt[0:L, 4 * q:4 * q + 4], decq[0:L, :, L - 1], [kD], ["wendt"])
                        cp(cdt[:, 4 * q:4 * q + 4], eAq[:, :, L - 1], [kE], ["cdt"])

                def s3(q):
                    eAq, decq, MTq, CTs, kE, kD, kM, kC = bufs_(q)
                    if q % 2 == 0:
                        ypcur[0], ypcur[1] = ps_next()
                    yp, ypk = ypcur
                    for i in range(4):
                        e_ = 4 * q + i
                        half = (e_ % 2) * 64
                        col = ((e_ // 2) % 4) * 128
                        o_ = yp[half:half + 64, col:col + L]
                        if sample:
                            mm(o_, xtok[0:L, e_ * 64:(e_ + 1) * 64], MTq[0:L, i, 0:L], True, True, ["nT", kM], [ypk])
                        else:
                            mm(o_, xtok[0:L, e_ * 64:(e_ + 1) * 64], MTq[0:L, i, 0:L], True, False, ["nT", kM], [ypk])
                            mm(o_, hTb[:, e_ * 64:(e_ + 1) * 64], CTs[:, i, 0:L], False, True, ["hTb", kC], [ypk])
                    if q % 2 == 1:
                        jb = (q // 2) * 4
                        act(yT[:, jb:jb + 4, cs_], yp[:, :].rearrange("p (a b) -> p a b", b=128)[:, :, 0:L], AF.Identity,
                            [ypk], ["yT"])
                s1(0)
                s1(1)
                for q in range(4):
                    s2(q)
                    for _ in range(2 if sample else 1):
                        if s5_todo:
                            s5_level2(s5_todo.pop(0))
                    s3(q)
                    if q + 2 < 4:
                        s1(q + 2)
                    if ch >= 1 and q in (0, 2) and len(s5_back_todo) > 2 and len(s5_todo) <= 4 - 0 and not s5_todo[:0]:
                        if all(t >= 4 for t in s5_todo):
                            s5_back_todo.pop(0)()
                tt(xw[0:L, :].rearrange("p (e c) -> p e c", c=64), xtok[0:L, :].rearrange("p (e c) -> p e c", c=64),
                   wendt[0:L, :].unsqueeze(2).broadcast_to([L, 16, 64]), ALU.mult, ["nT", "wendt"], ["nT"])
                if not sample:
                    tt(htmp[:, :].rearrange("p (e c) -> p e c", c=64), hT[:, :].rearrange("p (e c) -> p e c", c=64),
                       cdt[:, :].unsqueeze(2).broadcast_to([128, 16, 64]), ALU.mult, ["hT", "cdt"], ["htmp"])
                    for gr in range(2):
                        ps, pk = ps_next()
                        mm(ps[:, :], Btok[0:L, gr, :], xw[0:L, gr * 512:(gr + 1) * 512], True, True, ["Btok", "nT"], [pk])
                        tt(hT[:, gr * 512:(gr + 1) * 512], htmp[:, gr * 512:(gr + 1) * 512], ps[:, :], ALU.add,
                           ["htmp", pk], ["hT"])
                    act(hTb[:, :], hT[:, :], AF.Identity, ["hT"], ["hTb"])
                else:
                    yo_ps, yo_k = psb[3], "psb3"
                    xflat = xbc[:, :, :].rearrange("p a b -> p (a b)")
                    hT2, htmp2 = xflat[:, 0:1024], xflat[:, 1024:2048]
                    hTb2 = xflat[:, 2048:2560].bitcast(BF16)
                    S.add("dve", lambda e: e.memset(xflat[:, 0:1], 0.0), [], ["xbc", "hT2", "htmp2", "hTb2"])
                    sets = [(hT[:, :], hTb[:, :], htmp[:, :], "hT", "hTb", "htmp"), (hT2, hTb2, htmp2, "hT2", "hTb2", "htmp2")]
                    for sq_ in range(NSEQ_S):
                        h_f, h_b, h_t, kf, kb, kt = sets[sq_ % 2]
                        dma(h_f, d_h0T[sq_], ["d_h0T"], [kf])
                        act(h_b, h_f, AF.Identity, [kf], [kb])
                        for e_ in range(16):
                            half = (e_ % 2) * 64
                            o_ = yo_ps[half:half + 64, (e_ // 2) * 64 + sq_ * TS:(e_ // 2) * 64 + sq_ * TS + TS]
                            mm(o_, h_b[:, e_ * 64:(e_ + 1) * 64], yab[:, e_, sq_ * TS:(sq_ + 1) * TS], True, True,
                               [kb, "yab"], [yo_k])
                        tsc(Btokm[:, :, :], Btok[0:L, :, :], rowmask_s[:, sq_:sq_ + 1], None, ALU.mult, None,
                            ["Btok", "cs64"], ["Btokm"])
                        tt(h_t.rearrange("p (e c) -> p e c", c=64), h_f.rearrange("p (e c) -> p e c", c=64),
                           cds[:, :, sq_:sq_ + 1].broadcast_to([128, 16, 64]), ALU.mult, [kf, "cds"], [kt])
                        for gr in range(2):
                            ps, pk = ps_next()
                            mm(ps[:, :], Btokm[:, gr, :], xw[0:L, gr * 512:(gr + 1) * 512], True, True, ["Btokm", "nT"], [pk])
                            tt(h_t[:, gr * 512:(gr + 1) * 512], h_t[:, gr * 512:(gr + 1) * 512], ps[:, :], ALU.add,
                               [kt, pk], [kt])
                        out_dmas.append(dma(o_ssds[sq_], h_t, [kt], ["o_ssds"], cast=True))
                    act(yoff[:, :, :], yo_ps[:, :].rearrange("p (a b) -> p a b", b=WS), AF.Identity, [yo_k], ["Yall"])
                    tt(yT[:, :, 0:WS], yT[:, :, 0:WS], yoff[:, :, :], ALU.add, ["yT", "Yall"], ["yT"])
            if last_prompt:
                out_dmas.append(dma(o_ssdp[:, :], hT[:, :], ["hT"], ["o_ssdp"]))

            for j in range(8):
                stt(yT[:, j, 0:Wb], xs[:, j, 0:Wb], pv(PV_SD + j), yT[:, j, 0:Wb], ALU.mult, ALU.add,
                    ["xs", "pvec", "yT"], ["yT"])
                tt(yT[:, j, 0:Wb], yT[:, j, 0:Wb], gza[:, j, 0:Wb], ALU.mult, ["yT", "gza"], ["yT"])
            for gr in range(2):
                rms_stats(yT, "yT", Wb, 512.0, list(range(gr * 4, gr * 4 + 4)))
                if gr == 0 and not s5_todo and s5_back_todo:
                    s5_back_todo.pop(0)()
                for j in range(gr * 4, gr * 4 + 4):
                    stt(yab[:, j, 0:Wb], yT[:, j, 0:Wb], pv(PV_NG + j), rstd[:, 0:Wb], ALU.mult, ALU.mult,
                        ["yT", "pvec", "rstd"], ["yab"])

            for j in s5_todo:
                s5_level2(j)
            while s5_back_todo:
                s5_back_todo.pop(0)()
            for h in range(2):
                s5_back_evac(h)
            for j in range(8):
                stt(yT[:, j, 0:Wb], uT[:, j, 0:Wb], pv(PV_S5D + j), yT[:, j, 0:Wb], ALU.mult, ALU.add,
                    ["uT", "pvec", "yT"], ["yT"])
            ps_banks[0] = [0, 1, 2, 3, 4, 5, 6, 7]

            def ep_out(j, ps, pk):
                tt(xT[:, j, 0:Wb], ps[:, 0:Wb], xT[:, j, 0:Wb], ALU.add, [pk, "xT%d" % j], ["xT%d" % j])
            stream_matmul("wout", d_wout, 8, 8, wbuf, "wbuf", lambda k: yab[:, k, 0:Wb], ["yab"], Wb, ep_out, idx=lambda i: 2 * i)
            for j in range(8):
                cacc, cak = CA.nxt()
                csig, cgk = CG.nxt()
                act(cacc[:, 0:Wb], yT[:, j, 0:Wb], AF.Square, ["yT"], [cak])
                tsc(cacc[:, 0:Wb], cacc[:, 0:Wb], 0.044715, 1.0, ALU.mult, ALU.add, [cak], [cak])
                tt(cacc[:, 0:Wb], cacc[:, 0:Wb], yT[:, j, 0:Wb], ALU.mult, [cak, "yT"], [cak])
                act(csig[:, 0:Wb], cacc[:, 0:Wb], AF.Tanh, [cak], [cgk], scale=0.5 * GELU_C)
                stt(yT[:, j, 0:Wb], csig[:, 0:Wb], 1.0, yT[:, j, 0:Wb], ALU.add, ALU.mult, ["yT", cgk], ["yT"])
                cp(gbf[:, j, 0:Wb], yT[:, j, 0:Wb], ["yT"], ["xs"])

            def ep_glu(j, ps, pk):
                cacc, cak = CA.nxt()
                csig, cgk = CG.nxt()
                act(csig[:, 0:Wb], ps[:, 0:Wb], AF.Tanh, [pk, "glubh"], [cgk], bias=glubh[:, j:j + 1], scale=0.25)
                stt(cacc[:, 0:Wb], csig[:, 0:Wb], 1.0, yT[:, j, 0:Wb], ALU.add, ALU.mult, ["yT", cgk], [cak])
                stt(yab[:, 8 + j, 0:Wb], cacc[:, 0:Wb], 0.25, gzb[:, j, 0:Wb], ALU.mult, ALU.mult, [cak, "gzb"], ["yab"])
            stream_matmul("glu", d_glu, 8, 8, wbuf, "wbuf", lambda k: gbf[:, k, 0:Wb], ["xs"], Wb, ep_glu)

            stream_matmul("wout", d_wout, 8, 8, wbuf, "wbuf", lambda k: yab[:, 8 + k, 0:Wb], ["yab"], Wb, ep_out, idx=lambda i: 2 * i + 1)

            ps_banks[0] = [0, 1, 2, 3]

            def proj_ps(j):
                bk = 4 + j // 2
                return psb[bk][:, (j % 2) * 256:(j % 2) * 256 + 256], "psb%d" % bk
            stream_matmul("wproj", d_wproj, 8, 2, wbuf, "wbuf", lambda k: pTb[:, k, 0:Wb], ["pTb"], Wb, None, psf=proj_ps)
            rms_stats(xT, "xT%d", Wb, float(D), list(range(8)))
            for k in range(8):
                stt(nT[:, k, 0:Wb], xT[:, k, 0:Wb], pv(PV_GPLE + k), rstd[:, 0:Wb], ALU.mult, ALU.mult,
                    ["xT%d" % k, "pvec", "rstd"], ["nT"])

            def ep_gate(j, ps, pk):
                cacc, cak = CA.nxt()
                csig, cgk = CG.nxt()
                pp, ppk = proj_ps(j)
                act(csig[:, 0:Wb], ps[:, 0:Wb], AF.Tanh, [pk], [cgk], scale=0.5)
                stt(cacc[:, 0:Wb], csig[:, 0:Wb], 1.0, pp[:, 0:Wb], ALU.add, ALU.mult, [cgk, ppk], [cak])
                stt(xT[:, j, 0:Wb], cacc[:, 0:Wb], 0.5, xT[:, j, 0:Wb], ALU.mult, ALU.add, [cak, "xT%d" % j], ["xT%d" % j])
            stream_matmul("wgate", d_wgate, 8, 8, wbuf, "wbuf", lambda k: nT[:, k, 0:Wb], ["nT"], Wb, ep_gate)
            ps_banks[0] = [0, 1, 2, 3, 4, 5, 6, 7]

            rms_stats(xT, "xT%d", Wb, float(D), list(range(8)))
            for k in range(8):
                stt(yT[:, k, 0:Wb], xT[:, k, 0:Wb], pv(PV_GFIN + k), rstd[:, 0:Wb], ALU.mult, ALU.mult,
                    ["xT%d" % k, "pvec", "rstd"], ["yT"])
            for k in range(8):
                out_dmas.append(dma(o_yT[k * 128:(k + 1) * 128, c0:c0 + Wb], yT[:, k, 0:Wb], ["yT"], ["o_yT"]))

        nblk = NPB
        for b in range(nblk):
            do_block(b * WP, WP, False, b, b == NPB - 1)
            first_pass[0] = False
        do_block(SEQ, WS, True, NPB, False)

        S.add("sp", lambda e: e.nop(), extra_deps=set(out_dmas))

        with nc.Block() as block:
            S.emit(nc, block, esem, dsems)
        build_program.stats = S.stats
    return nc


def _consts():
    cf = np.zeros((128, 1024), np.float32)
    cf[:, 0:128] = 1.0
    cf[:, 128:256] = np.eye(128, dtype=np.float32)
    cf[0:64, 768] = 1.0
    cf[64:128, 768] = -1.0
    cf[:, 769] = math.pi / 2.0
    cf[:, 770] = EPS
    cf[:, 771] = 1.0
    sj = np.arange(128) // 16
    cf[:, 772:900] = (sj[None, :] >= sj[:, None]).astype(np.float32)
    cf[:, 900:933] = np.arange(33, dtype=np.float32)[None, :]
    c16 = np.zeros((16, 836), np.float32)
    bi = np.zeros((16, 4, 128), np.float32)
    oq = np.zeros((16, 4, 128), np.float32)
    for e in range(16):
        bi[e, e % 4, :] = 1.0
        oq[e, e // 4, :] = 1.0
        c16[e, 512 + e // 4] = 1.0
    cf[0:16, 256:768] = bi.reshape(16, 512)
    c16[:, 0:512] = oq.reshape(16, 512)
    rp = np.ones(256, np.float32)
    rp[0] = 0.0
    rp[128] = 0.0
    c16[:, 516:772] = rp[None, :]
    c16[:, 772:836] = (np.arange(64) % TS != 0).astype(np.float32)[None, :]
    cb = np.zeros((128, 1024), np.float32)
    cb[:, 896:1024] = 1.0
    cb[:, 0:128] = np.eye(128, dtype=np.float32)
    s_ = np.arange(128)[:, None]
    t_ = np.arange(128)[None, :]
    mp = np.where(s_ <= t_, 0.0, NEG).astype(np.float32)
    cb[:, 128:640] = np.tile(mp, (1, 4))
    s6 = np.arange(64)[:, None]
    t6 = np.arange(64)[None, :]
    ms = np.where((s6 <= t6) & (s6 // TS == t6 // TS), 0.0, NEG).astype(np.float32)
    cb[0:64, 640:896] = np.tile(ms, (1, 4))
    cs = np.zeros((64, 80), np.float32)
    cs[:, 0:64] = (t6 == (s6 // TS) * TS + TS - 1).astype(np.float32)
    cs[:, 64:80] = (s6 // TS == np.arange(16)[None, :]).astype(np.float32)
    return cf, c16, cb, cs


def _wblocks(w, nk):
    ncols = w.shape[1]
    return np.ascontiguousarray(w.reshape(nk, 128, ncols // 128, 128).transpose(2, 1, 0, 3))


def _chan(v):
    return np.ascontiguousarray(np.asarray(v, np.float32).reshape(8, 128).T)


_PROG = {}


def kernel(x_prompt, x_sample, p_prompt, p_sample, state_ssd, state_conv, state_s5_re, state_s5_im,
           w_in, g_in, conv_w, conv_b, dt_bias, a_log, ssd_d, ssd_norm_g,
           s5_lambda_re, s5_lambda_im, s5_log_dt, s5_b_re, s5_b_im, s5_c_re, s5_c_im, s5_d,
           glu_w, glu_b, w_out, g_ple, w_ple_gate, w_ple_proj, g_final):
    f = lambda a: np.asarray(a, np.float32)
    x_prompt, x_sample, p_prompt, p_sample = f(x_prompt), f(x_sample), f(p_prompt)[0], f(p_sample)[0]
    state_ssd, state_conv = f(state_ssd)[0], f(state_conv)[0]
    s5re, s5im = f(state_s5_re)[0], f(state_s5_im)[0]
    w_in0 = f(w_in)[0]
    wcols = np.zeros((1024, NBLK_IN * 128), np.float32)
    wcols[:, 0:2560] = w_in0[:, 0:2560]
    wcols[:, 2560:2576] = w_in0[:, 2560:2576]
    wcols[:, 2688:4736] = w_in0[:, 2576:4624]
    w_in_l = _wblocks(wcols, 8)
    glu_l = _wblocks(f(glu_w)[0], 8)
    wo = _wblocks(f(w_out)[0], 16)
    wout_l = np.ascontiguousarray(wo.reshape(8, 128, 2, 8, 128).transpose(0, 2, 1, 3, 4).reshape(16, 128, 8, 128))
    wgate_l = _wblocks(f(w_ple_gate)[0], 8)
    wproj_l = _wblocks(f(w_ple_proj)[0], 2)
    pvec = np.zeros((128, PV_N), np.float32)
    pvec[:, PV_GIN:PV_GIN + 8] = _chan(f(g_in)[0])
    cbv = f(conv_b)[0].reshape(12, 128).T
    pvec[:, PV_CB:PV_CB + 12] = cbv
    cw = f(conv_w)[0].reshape(4, 12, 128).transpose(2, 1, 0)
    pvec[:, PV_CW:PV_CW + 48] = cw.reshape(128, 48)
    pvec[:, PV_SD:PV_SD + 8] = _chan(np.repeat(f(ssd_d)[0], 64))
    pvec[:, PV_NG:PV_NG + 8] = _chan(f(ssd_norm_g)[0])
    pvec[:, PV_S5D:PV_S5D + 8] = _chan(f(s5_d)[0].reshape(-1))
    pvec[:, PV_GB:PV_GB + 8] = _chan(f(glu_b)[0])
    pvec[:, PV_GPLE:PV_GPLE + 8] = _chan(f(g_ple)[0])
    pvec[:, PV_GFIN:PV_GFIN + 8] = _chan(f(g_final))
    p16 = np.stack([f(dt_bias)[0], f(a_log)[0]], axis=1).astype(np.float32)
    lre, lim, ldt = f(s5_lambda_re)[0], f(s5_lambda_im)[0], f(s5_log_dt)[0]
    bre, bim = f(s5_b_re)[0], f(s5_b_im)[0]
    cre, cim = f(s5_c_re)[0], f(s5_c_im)[0]
    def layC(arr_gp):
        a_ = np.asarray(arr_gp, np.float32)
        a_ = a_.reshape((32, 2) + a_.shape[1:])
        a_ = np.moveaxis(a_, 0, 2)
        return np.ascontiguousarray(a_.reshape((128, 32) + a_.shape[3:]))
    lamC = np.stack([layC(lre), layC(lim), layC(np.repeat(ldt[:, None], 64, axis=1))], axis=1)
    bC = np.stack([layC(bre), layC(bim)], axis=1).reshape(128, -1)
    cC = np.stack([layC(cre.transpose(0, 2, 1)), layC(cim.transpose(0, 2, 1))], axis=1).reshape(128, -1)
    kv2 = np.zeros((128, 2, 16, 32), np.float32)
    kv2[:, 0] = (np.arange(16) - 7).astype(np.float32)[None, :, None]
    kv2[:, 1] = (8 - np.arange(16)).astype(np.float32)[None, :, None]
    kv2 = kv2.reshape(128, -1)
    fsel = np.zeros((128, 4, 8, 128), np.float32)
    for hf_ in range(2):
        for a_ in range(4):
            for b_ in range(8):
                for h_ in range(16):
                    fsel[hf_ * 64 + a_ * 16 + h_, a_, b_, b_ * 16 + h_] = 1.0
    fsel = fsel.reshape(128, -1)
    cf, c16, cb, cs = _consts()

    in_maps = []
    for c in range(NCORES):
        qs = slice(c * NSEQ_S, (c + 1) * NSEQ_S)
        xT = np.concatenate([x_prompt[c].T, x_sample[qs].reshape(WS, D).T], axis=1)
        pT = np.concatenate([p_prompt[c].T, p_sample[qs].reshape(WS, 256).T], axis=1)
        h0T = state_ssd[qs].reshape(NSEQ_S, 1024, 128).transpose(0, 2, 1)
        chist = state_conv[qs].reshape(NSEQ_S, 3, 12, 128).transpose(3, 2, 0, 1)
        winitC = np.stack([layC(s5re[qs].transpose(1, 2, 0)), layC(s5im[qs].transpose(1, 2, 0))], axis=1).reshape(128, -1)
        in_maps.append({
            "xT": np.ascontiguousarray(xT), "pT": np.ascontiguousarray(pT),
            "w_in_l": w_in_l, "glu_l": glu_l, "wout_l": wout_l, "wgate_l": wgate_l, "wproj_l": wproj_l,
            "pvec": pvec, "p16": p16,
            "h0T": np.ascontiguousarray(h0T), "chist": np.ascontiguousarray(chist),
            "winitC": np.ascontiguousarray(winitC),
            "s5_lamC": lamC, "s5_bC": bC, "s5_cC": cC, "kv2": kv2, "fsel": fsel,
            "cf": cf, "c16": c16, "cb": cb, "cs64": cs,
        })
    if "nc" not in _PROG:
        _PROG["nc"] = build_program()
    res = run_bass_kernel_spmd(_PROG["nc"], in_maps, core_ids=list(range(NCORES)))
    R = res.results

    y_prompt = np.stack([R[c]["yT"][:, 0:SEQ].T for c in range(NCORES)], axis=0)
    y_sample = np.concatenate([R[c]["yT"][:, SEQ:].T.reshape(NSEQ_S, TS, D) for c in range(NCORES)], axis=0)
    ssd_p = np.stack([R[c]["ssd_p"].T.reshape(16, 64, 128) for c in range(NCORES)], axis=0)[None]
    ssd_s = np.concatenate([R[c]["ssd_s"].transpose(0, 2, 1).reshape(NSEQ_S, 16, 64, 128) for c in range(NCORES)], axis=0)[None]
    conv_p = np.stack([R[c]["conv_p"].transpose(2, 1, 0).reshape(3, 1536) for c in range(NCORES)], axis=0)[None]
    conv_s = np.concatenate([R[c]["conv_s"].transpose(2, 3, 1, 0).reshape(NSEQ_S, 3, 1536) for c in range(NCORES)], axis=0)[None]
    def unC(x):
        x = x.reshape((2, 64, 32) + x.shape[2:])
        x = np.moveaxis(x, 2, 0)
        return x.reshape((64, 64) + x.shape[3:])
    re_p = np.stack([unC(R[c]["s5_p"].reshape(128, 2, 32)[:, 0]) for c in range(NCORES)], axis=0)[None]
    im_p = np.stack([unC(R[c]["s5_p"].reshape(128, 2, 32)[:, 1]) for c in range(NCORES)], axis=0)[None]
    re_s = np.concatenate([unC(R[c]["s5_s"].reshape(128, 2, 32, NSEQ_S)[:, 0]).transpose(2, 0, 1) for c in range(NCORES)], axis=0)[None]
    im_s = np.concatenate([unC(R[c]["s5_s"].reshape(128, 2, 32, NSEQ_S)[:, 1]).transpose(2, 0, 1) for c in range(NCORES)], axis=0)[None]
    c_ = np.ascontiguousarray
    return (c_(y_prompt.astype(np.float32)), c_(y_sample.astype(np.float32)), c_(ssd_p), c_(conv_p), c_(re_p), c_(im_p),
            c_(ssd_s), c_(conv_s), c_(re_s), c_(im_s))
```
